# Optimizing a Trainium2 kernel written in Bass

```python
import math
import jax
import jax.numpy as jnp
from jax import lax
import numpy as np

D_MODEL = 2048
BATCH = 4
SEQ = 4096
DEPTH = 2

GRID_W = 64
CTX_LEN = 256
N_EVEN = (DEPTH + 1) // 2
N_ODD = DEPTH // 2
N_MOD = 6
Q_BLOCK = 128
ROPE_THETA = 10000.0
EPS = 1e-6

MLA_HEADS = 8
MLA_Q_RANK = 512
MLA_KV_RANK = 512
MLA_NOPE = 128
MLA_ROPE = 64
MLA_V = 128
MLA_QK = MLA_NOPE + MLA_ROPE
GDN_HEADS = 8
GDN_DK = 128
GDN_DV = 128
GDN_CONV = 5
GDN_CHUNK = 64
GDN_QKV = GDN_HEADS * (2 * GDN_DK + GDN_DV)
DIFF_HEADS = 8
DIFF_D = 64
DIFF_DV = 2 * DIFF_D
GLA_HEADS = 4
GLA_DK = 128
GLA_DV = 256
GLA_GATE_RANK = 16
GLA_GATE_NORM = 16.0
GLA_CHUNK = 64
GLA_SUB = 16
FFN_HIDDEN = -(-8 * D_MODEL // (3 * 256)) * 256

AB_SPLITS = (MLA_Q_RANK, MLA_KV_RANK, MLA_ROPE, GDN_QKV, GDN_HEADS * GDN_DV, 2 * GDN_HEADS, 2 * GDN_HEADS)
CD_SPLITS = (DIFF_HEADS * 2 * DIFF_D, DIFF_HEADS * 2 * DIFF_D, DIFF_HEADS * DIFF_DV,
             GLA_HEADS * GLA_DK, GLA_HEADS * GLA_DK, GLA_HEADS * GLA_DV, GLA_HEADS * GLA_DV, 2 * GLA_GATE_RANK)
AB_IN = sum(AB_SPLITS)
CD_IN = sum(CD_SPLITS)
AB_OUT = MLA_HEADS * MLA_V + GDN_HEADS * GDN_DV
CD_OUT = DIFF_HEADS * DIFF_DV + GLA_HEADS * GLA_DV

kernel_name = 'hybrid_mla_gdn_diff_gla_dit'


def rms_norm(x, g):
    xf = x.astype(jnp.float32)
    y = xf * lax.rsqrt(jnp.mean(xf * xf, axis=-1, keepdims=True) + EPS)
    return (y * g.astype(jnp.float32)).astype(x.dtype)


def l2_norm(x):
    xf = x.astype(jnp.float32)
    return (xf * lax.rsqrt(jnp.sum(xf * xf, axis=-1, keepdims=True) + EPS)).astype(x.dtype)


def split_cols(z, sizes):
    return jnp.split(z, np.cumsum(sizes)[:-1].tolist(), axis=-1)


def to_heads(a, n, d):
    return a.reshape(a.shape[:2] + (n, d))


def flat_heads(a):
    return a.reshape(a.shape[:2] + (-1,))


def modulate(x, g, shift, scale):
    return rms_norm(x, g) * (1 + scale) + shift


def swiglu(h, w_gate, w_up, w_down):
    return (jax.nn.silu(h @ w_gate) * (h @ w_up)) @ w_down


def axial_rope_tables(n_tokens, rot_dim):
    rows = n_tokens // GRID_W
    row = jnp.repeat(jnp.arange(rows, dtype=jnp.float32), GRID_W)
    col = jnp.tile(jnp.arange(GRID_W, dtype=jnp.float32), rows)
    axis_dim = rot_dim // 2
    inv_freq = ROPE_THETA ** (-jnp.arange(0, axis_dim, 2, dtype=jnp.float32) / axis_dim)
    ang_r = row[:, None] * inv_freq[None, :]
    ang_c = col[:, None] * inv_freq[None, :]
    return jnp.cos(ang_r), jnp.sin(ang_r), jnp.cos(ang_c), jnp.sin(ang_c)


def _rotate_half(x, cos, sin):
    x1, x2 = jnp.split(x, 2, axis=-1)
    return jnp.concatenate([x1 * cos - x2 * sin, x2 * cos + x1 * sin], axis=-1)


def apply_axial_rope(x, tabs):
    cos_r, sin_r, cos_c, sin_c = tabs
    shape = (1, x.shape[1]) + (1,) * (x.ndim - 3) + (cos_r.shape[-1],)
    xr, xc = jnp.split(x, 2, axis=-1)
    return jnp.concatenate([
        _rotate_half(xr, cos_r.reshape(shape).astype(x.dtype), sin_r.reshape(shape).astype(x.dtype)),
        _rotate_half(xc, cos_c.reshape(shape).astype(x.dtype), sin_c.reshape(shape).astype(x.dtype))], axis=-1)


def sweep_query_blocks(fn, *qs):
    bsz, t = qs[0].shape[:2]
    nb = t // Q_BLOCK
    blocks = tuple(jnp.moveaxis(q.reshape((bsz, nb, Q_BLOCK) + q.shape[2:]), 1, 0) for q in qs)
    out = lax.map(lambda blk: fn(*blk), blocks)
    out = jnp.moveaxis(out, 0, 1)
    return out.reshape((bsz, t) + out.shape[3:])


def softmax_attention(q, k, v, scale):
    def block(qb):
        s = jnp.einsum('bqhd,bkhd->bhqk', qb, k).astype(jnp.float32) * scale
        p = jax.nn.softmax(s, axis=-1).astype(v.dtype)
        return jnp.einsum('bhqk,bkhd->bqhd', p, v)
    return sweep_query_blocks(block, q)


def differential_attention(q, k, v, lam, scale):
    def block(qb):
        s = jnp.einsum('bqhmd,bkhmd->bhmqk', qb, k).astype(jnp.float32) * scale
        p = jax.nn.softmax(s, axis=-1)
        a = (p[:, :, 0] - lam * p[:, :, 1]).astype(v.dtype)
        return jnp.einsum('bhqk,bkhd->bqhd', a, v)
    return sweep_query_blocks(block, q)


def depthwise_conv_centred(x, w):
    taps, ch = w.shape
    pad = taps // 2
    return lax.conv_general_dilated(x, w[:, None, :].astype(x.dtype), window_strides=(1,), padding=[(pad, pad)],
                                    dimension_numbers=('NWC', 'WIO', 'NWC'), feature_group_count=ch)


def gated_delta_chunked(q, k, v, g, beta, s0, with_output):
    bsz, t, h, dk = k.shape
    dv = v.shape[-1]
    c = GDN_CHUNK
    n = t // c
    f32 = jnp.float32

    def chunks(a):
        a = a.astype(f32).reshape((bsz, n, c, h) + a.shape[3:])
        return jnp.moveaxis(a, 3, 1)

    kc, vc, gc, bc = chunks(k), chunks(v), chunks(g), chunks(beta)
    gcum = jnp.cumsum(gc, axis=-1)
    idx = jnp.arange(c)
    incl = idx[:, None] >= idx[None, :]
    strict = idx[:, None] > idx[None, :]
    decay = jnp.where(incl, jnp.exp(jnp.where(incl, gcum[..., :, None] - gcum[..., None, :], 0.0)), 0.0)
    kb = kc * bc[..., None]
    lower = jnp.where(strict, jnp.einsum('bhnid,bhnjd->bhnij', kb, kc) * decay, 0.0)
    rhs = jnp.concatenate([vc * bc[..., None], kb * jnp.exp(gcum)[..., None]], axis=-1)
    sol = lax.linalg.triangular_solve(lower + jnp.eye(c, dtype=f32), rhs, left_side=True, lower=True,
                                      unit_diagonal=True)
    u, w = sol[..., :dv], sol[..., dv:]
    g_last = gcum[..., -1]
    k_end = kc * jnp.exp(g_last[..., None] - gcum)[..., None]
    xs = (u, w, k_end, jnp.exp(g_last))
    if with_output:
        qc = chunks(q)
        qk = jnp.where(incl, jnp.einsum('bhnid,bhnjd->bhnij', qc, kc) * decay, 0.0)
        xs = xs + (qc * jnp.exp(gcum)[..., None], qk)
    xs = tuple(jnp.moveaxis(a, 2, 0) for a in xs)

    def step(state, inp):
        u_n, w_n, kend_n, dl_n = inp[:4]
        v_new = u_n - jnp.einsum('bhcd,bhde->bhce', w_n, state)
        new_state = state * dl_n[..., None, None] + jnp.einsum('bhcd,bhce->bhde', kend_n, v_new)
        if not with_output:
            return new_state, None
        qd_n, qk_n = inp[4:]
        out = jnp.einsum('bhcd,bhde->bhce', qd_n, state) + jnp.einsum('bhij,bhje->bhie', qk_n, v_new)
        return new_state, out

    final, out = lax.scan(step, s0, xs)
    if not with_output:
        return None, final
    out = jnp.moveaxis(jnp.moveaxis(out, 0, 2), 1, 3).reshape(bsz, t, h, dv)
    return out, final


def gla_chunked(q, k, v, glog, s0, with_output):
    bsz, t, h, dk = k.shape
    dv = v.shape[-1]
    c, sub = GLA_CHUNK, GLA_SUB
    n, ns = t // c, c // sub
    f32 = jnp.float32

    def chunks(a):
        return a.astype(f32).reshape(bsz, n, c, h, a.shape[-1]).transpose(1, 0, 3, 2, 4)

    earlier = jnp.arange(c)[None, :] < (jnp.arange(ns) * sub)[:, None]
    tril = jnp.arange(sub)[:, None] >= jnp.arange(sub)[None, :]
    xs = (chunks(k), chunks(v), chunks(glog)) + ((chunks(q),) if with_output else ())

    def step(state, inp):
        k_n, v_n, g_n = inp[:3]
        cum = jnp.cumsum(g_n, axis=-2)
        last = cum[:, :, -1, :]
        new_state = state * jnp.exp(last)[..., None] + jnp.einsum(
            'bhcd,bhce->bhde', k_n * jnp.exp(last[:, :, None, :] - cum), v_n)
        if not with_output:
            return new_state, None
        q_n = inp[3]
        out = jnp.einsum('bhcd,bhde->bhce', q_n * jnp.exp(cum), state)
        cs = cum.reshape(bsz, h, ns, sub, dk)
        qs = q_n.reshape(bsz, h, ns, sub, dk)
        ks = k_n.reshape(bsz, h, ns, sub, dk)
        vs = v_n.reshape(bsz, h, ns, sub, dv)
        ref = jnp.concatenate([jnp.zeros_like(cs[:, :, :1, 0]), cs[:, :, :-1, -1]], axis=2)
        q_ref = qs * jnp.exp(cs - ref[:, :, :, None, :])
        k_ref = jnp.where(earlier[..., None],
                          jnp.exp(jnp.minimum(ref[:, :, :, None, :] - cum[:, :, None, :, :], 0.0)), 0.0) * k_n[:, :, None]
        a_cross = jnp.einsum('bhsid,bhsjd->bhsij', q_ref, k_ref)
        dec = jnp.where(tril[..., None], jnp.exp(jnp.minimum(cs[..., :, None, :] - cs[..., None, :, :], 0.0)), 0.0)
        a_local = jnp.einsum('bhsid,bhsjd,bhsijd->bhsij', qs, ks, dec)
        local = jnp.einsum('bhsij,bhje->bhsie', a_cross, v_n) + jnp.einsum('bhsij,bhsje->bhsie', a_local, vs)
        return new_state, out + local.reshape(bsz, h, c, dv)

    final, out = lax.scan(step, s0, xs)
    if not with_output:
        return None, final
    return out.transpose(1, 0, 3, 2, 4).reshape(bsz, t, h, dv), final


def _direction_inputs(stream, d):
    shared, per_dir = stream
    seq = tuple(shared) + tuple(p[:, :, d] for p in per_dir)
    return seq if d == 0 else tuple(jnp.flip(a, axis=1) for a in seq)


def bidirectional_scan(scan_fn, lat, ctx, state_shape, need_ctx):
    outs_l, outs_c = [], []
    for d in range(2):
        out_c, state_c = scan_fn(*_direction_inputs(ctx, d), jnp.zeros(state_shape, jnp.float32), need_ctx)
        out_l, _ = scan_fn(*_direction_inputs(lat, d), state_c, True)
        if d == 1:
            out_l = jnp.flip(out_l, axis=1)
            out_c = jnp.flip(out_c, axis=1) if need_ctx else None
        outs_l.append(out_l)
        outs_c.append(out_c)
    return outs_l[0] + outs_l[1], (outs_c[0] + outs_c[1] if need_ctx else None)


def mla_queries(q_a, q_a_norm, w_uq, q_norm, tabs):
    q = to_heads(rms_norm(q_a, q_a_norm) @ w_uq, MLA_HEADS, MLA_QK)
    q = rms_norm(q, q_norm)
    if tabs is None:
        return q
    return jnp.concatenate([q[..., :MLA_NOPE], apply_axial_rope(q[..., MLA_NOPE:], tabs)], axis=-1)


def mla_keys_values(kv_a, k_rope, kv_a_norm, w_ukv, k_norm, tabs):
    bsz, t, _ = kv_a.shape
    kv = to_heads(rms_norm(kv_a, kv_a_norm) @ w_ukv, MLA_HEADS, MLA_NOPE + MLA_V)
    k_nope, v = kv[..., :MLA_NOPE], kv[..., MLA_NOPE:]
    k = jnp.concatenate([k_nope, jnp.broadcast_to(k_rope[:, :, None, :], (bsz, t, MLA_HEADS, MLA_ROPE))], axis=-1)
    k = rms_norm(k, k_norm)
    if tabs is not None:
        k = jnp.concatenate([k[..., :MLA_NOPE], apply_axial_rope(k[..., MLA_NOPE:], tabs)], axis=-1)
    return k, v


def gdn_inputs(qkv, a_raw, b_raw, conv_w, a_log, dt_bias):
    bsz, t, _ = qkv.shape
    qkv = jax.nn.silu(depthwise_conv_centred(qkv, conv_w))
    q, k, v = split_cols(qkv, (GDN_HEADS * GDN_DK, GDN_HEADS * GDN_DK, GDN_HEADS * GDN_DV))
    q = l2_norm(to_heads(q, GDN_HEADS, GDN_DK)) * (GDN_DK ** -0.5)
    k = l2_norm(to_heads(k, GDN_HEADS, GDN_DK))
    v = to_heads(v, GDN_HEADS, GDN_DV)
    a = a_raw.astype(jnp.float32).reshape(bsz, t, 2, GDN_HEADS)
    g = -jnp.exp(a_log.astype(jnp.float32)) * jax.nn.softplus(a + dt_bias.astype(jnp.float32))
    beta = jax.nn.sigmoid(b_raw.astype(jnp.float32).reshape(bsz, t, 2, GDN_HEADS))
    return (q, k, v), (g, beta)


def gla_inputs(q, k, v, lowrank, gate_w2, gate_b2):
    bsz, t, _ = q.shape
    q = to_heads(q, GLA_HEADS, GLA_DK) * (GLA_DK ** -0.5)
    k = to_heads(k, GLA_HEADS, GLA_DK)
    v = to_heads(v, GLA_HEADS, GLA_DV)
    lr = lowrank.reshape(bsz, t, 2, GLA_GATE_RANK)
    logits = jnp.einsum('btdr,drk->btdk', lr, gate_w2).astype(jnp.float32) + gate_b2.astype(jnp.float32)
    glog = (jax.nn.log_sigmoid(logits) / GLA_GATE_NORM).reshape(bsz, t, 2, GLA_HEADS, GLA_DK)
    return (q, k, v), (glog,)


def mla_gdn_mixer(h, hc, w_in, q_a_norm, w_uq, kv_a_norm, w_ukv, q_norm, k_norm,
                  conv_w, a_log, dt_bias, out_norm, w_out, need_ctx):
    bsz, t, _ = h.shape
    tabs = axial_rope_tables(t, MLA_ROPE)
    q_a, kv_a, k_rope, qkv, z, a_raw, b_raw = split_cols(h @ w_in, AB_SPLITS)
    cq_a, ckv_a, ck_rope, cqkv, cz, ca_raw, cb_raw = split_cols(hc @ w_in, AB_SPLITS)
    k_l, v_l = mla_keys_values(kv_a, k_rope, kv_a_norm, w_ukv, k_norm, tabs)
    k_c, v_c = mla_keys_values(ckv_a, ck_rope, kv_a_norm, w_ukv, k_norm, None)
    q_l = mla_queries(q_a, q_a_norm, w_uq, q_norm, tabs)
    scale = MLA_QK ** -0.5
    att_l = softmax_attention(q_l, jnp.concatenate([k_c, k_l], axis=1), jnp.concatenate([v_c, v_l], axis=1), scale)
    rec_l, rec_c = bidirectional_scan(gated_delta_chunked,
                                      gdn_inputs(qkv, a_raw, b_raw, conv_w, a_log, dt_bias),
                                      gdn_inputs(cqkv, ca_raw, cb_raw, conv_w, a_log, dt_bias),
                                      (bsz, GDN_HEADS, GDN_DK, GDN_DV), need_ctx)
    rec_l = rms_norm(rec_l.astype(h.dtype), out_norm) * jax.nn.silu(to_heads(z, GDN_HEADS, GDN_DV))
    y = jnp.concatenate([flat_heads(att_l), flat_heads(rec_l)], axis=-1) @ w_out
    if not need_ctx:
        return y, None
    q_c = mla_queries(cq_a, q_a_norm, w_uq, q_norm, None)
    att_c = softmax_attention(q_c, k_c, v_c, scale)
    rec_c = rms_norm(rec_c.astype(hc.dtype), out_norm) * jax.nn.silu(to_heads(cz, GDN_HEADS, GDN_DV))
    yc = jnp.concatenate([flat_heads(att_c), flat_heads(rec_c)], axis=-1) @ w_out
    return y, yc


def diff_qk(a, norm_g, tabs):
    t = rms_norm(a.reshape(a.shape[:2] + (DIFF_HEADS, 2, DIFF_D)), norm_g)
    return t if tabs is None else apply_axial_rope(t, tabs)


def diff_gla_mixer(h, hc, w_in, q_norm, k_norm, lambdas, sub_norm, gate_w2, gate_b2, out_norm, w_out,
                   lam_init, need_ctx):
    bsz, t, _ = h.shape
    tabs = axial_rope_tables(t, DIFF_D)
    dq, dk, dv, gq, gk, gv, gg, glr = split_cols(h @ w_in, CD_SPLITS)
    cdq, cdk, cdv, cgq, cgk, cgv, cgg, cglr = split_cols(hc @ w_in, CD_SPLITS)
    lam_p = lambdas.astype(jnp.float32)
    lam = jnp.exp(jnp.sum(lam_p[0] * lam_p[1])) - jnp.exp(jnp.sum(lam_p[2] * lam_p[3])) + lam_init
    scale = DIFF_D ** -0.5
    k_l, k_c = diff_qk(dk, k_norm, tabs), diff_qk(cdk, k_norm, None)
    v_l, v_c = to_heads(dv, DIFF_HEADS, DIFF_DV), to_heads(cdv, DIFF_HEADS, DIFF_DV)
    q_l = diff_qk(dq, q_norm, tabs)
    att_l = differential_attention(q_l, jnp.concatenate([k_c, k_l], axis=1), jnp.concatenate([v_c, v_l], axis=1),
                                   lam, scale)
    att_l = rms_norm(att_l, sub_norm) * (1.0 - lam_init)
    rec_l, rec_c = bidirectional_scan(gla_chunked,
                                      gla_inputs(gq, gk, gv, glr, gate_w2, gate_b2),
                                      gla_inputs(cgq, cgk, cgv, cglr, gate_w2, gate_b2),
                                      (bsz, GLA_HEADS, GLA_DK, GLA_DV), need_ctx)
    rec_l = rms_norm(rec_l.astype(h.dtype), out_norm) * jax.nn.silu(to_heads(gg, GLA_HEADS, GLA_DV))
    y = jnp.concatenate([flat_heads(att_l), flat_heads(rec_l)], axis=-1) @ w_out
    if not need_ctx:
        return y, None
    q_c = diff_qk(cdq, q_norm, None)
    att_c = rms_norm(differential_attention(q_c, k_c, v_c, lam, scale), sub_norm) * (1.0 - lam_init)
    rec_c = rms_norm(rec_c.astype(hc.dtype), out_norm) * jax.nn.silu(to_heads(cgg, GLA_HEADS, GLA_DV))
    yc = jnp.concatenate([flat_heads(att_c), flat_heads(rec_c)], axis=-1) @ w_out
    return y, yc


def setup_inputs(seed: int = 0) -> dict:
    key = jax.random.key(seed)
    keys = iter(jax.random.split(key, 40))
    f32 = jnp.float32

    def normal(shape, scale):
        return jax.random.normal(next(keys), shape, f32) * scale

    def gain(shape):
        return 1.0 + normal(shape, 0.02)

    d = D_MODEL
    dt = jnp.exp(jax.random.uniform(next(keys), (N_EVEN, 2, GDN_HEADS), f32, math.log(1e-3), math.log(1e-1)))
    return {
        'x': normal((BATCH, SEQ, d), 1.0),
        'c': normal((BATCH, d), 1.0),
        'ctx': normal((BATCH, CTX_LEN, d), 1.0),
        'c_ctx': normal((d,), 1.0),
        'ada_w': normal((DEPTH, d, N_MOD * d), 0.5 * d ** -0.5),
        'ada_b': normal((DEPTH, N_MOD * d), 0.02),
        'norm_mix_g': gain((DEPTH, d)),
        'norm_ffn_g': gain((DEPTH, d)),
        'ffn_w_gate': normal((DEPTH, d, FFN_HIDDEN), d ** -0.5),
        'ffn_w_up': normal((DEPTH, d, FFN_HIDDEN), d ** -0.5),
        'ffn_w_down': normal((DEPTH, FFN_HIDDEN, d), FFN_HIDDEN ** -0.5),
        'ab_w_in': normal((N_EVEN, d, AB_IN), d ** -0.5),
        'mla_q_a_norm': gain((N_EVEN, MLA_Q_RANK)),
        'mla_w_uq': normal((N_EVEN, MLA_Q_RANK, MLA_HEADS * MLA_QK), MLA_Q_RANK ** -0.5),
        'mla_kv_a_norm': gain((N_EVEN, MLA_KV_RANK)),
        'mla_w_ukv': normal((N_EVEN, MLA_KV_RANK, MLA_HEADS * (MLA_NOPE + MLA_V)), MLA_KV_RANK ** -0.5),
        'mla_q_norm': gain((N_EVEN, MLA_QK)),
        'mla_k_norm': gain((N_EVEN, MLA_QK)),
        'gdn_conv_w': normal((N_EVEN, GDN_CONV, GDN_QKV), GDN_CONV ** -0.5),
        'gdn_a_log': jnp.log(jax.random.uniform(next(keys), (N_EVEN, 2, GDN_HEADS), f32, 1.0, 16.0)),
        'gdn_dt_bias': dt + jnp.log(-jnp.expm1(-dt)),
        'gdn_out_norm': gain((N_EVEN, GDN_DV)),
        'ab_w_out': normal((N_EVEN, AB_OUT, d), AB_OUT ** -0.5),
        'cd_w_in': normal((N_ODD, d, CD_IN), d ** -0.5),
        'diff_q_norm': gain((N_ODD, 2, DIFF_D)),
        'diff_k_norm': gain((N_ODD, 2, DIFF_D)),
        'diff_lambda': normal((N_ODD, 4, DIFF_D), 0.1),
        'diff_sub_norm': gain((N_ODD, DIFF_DV)),
        'gla_gate_w2': normal((N_ODD, 2, GLA_GATE_RANK, GLA_HEADS * GLA_DK), GLA_GATE_RANK ** -0.5),
        'gla_gate_b2': normal((N_ODD, 2, GLA_HEADS * GLA_DK), 0.1),
        'gla_out_norm': gain((N_ODD, GLA_DV)),
        'cd_w_out': normal((N_ODD, CD_OUT, d), CD_OUT ** -0.5),
    }


def reference(x, c, ctx, c_ctx, ada_w, ada_b, norm_mix_g, norm_ffn_g, ffn_w_gate, ffn_w_up, ffn_w_down,
              ab_w_in, mla_q_a_norm, mla_w_uq, mla_kv_a_norm, mla_w_ukv, mla_q_norm, mla_k_norm,
              gdn_conv_w, gdn_a_log, gdn_dt_bias, gdn_out_norm, ab_w_out,
              cd_w_in, diff_q_norm, diff_k_norm, diff_lambda, diff_sub_norm,
              gla_gate_w2, gla_gate_b2, gla_out_norm, cd_w_out):
    bsz = x.shape[0]
    for layer in range(DEPTH):
        last = layer == DEPTH - 1
        need_ctx = not last
        mod = (jax.nn.silu(c) @ ada_w[layer] + ada_b[layer]).reshape(bsz, N_MOD, 1, D_MODEL)
        mod_c = (jax.nn.silu(c_ctx) @ ada_w[layer] + ada_b[layer]).reshape(N_MOD, 1, D_MODEL)
        shift1, scale1, gate1, shift2, scale2, gate2 = (mod[:, i] for i in range(N_MOD))
        cshift1, cscale1, cgate1, cshift2, cscale2, cgate2 = (mod_c[i] for i in range(N_MOD))
        h = modulate(x, norm_mix_g[layer], shift1, scale1)
        hc = modulate(ctx, norm_mix_g[layer], cshift1, cscale1)
        i = layer // 2
        if layer % 2 == 0:
            y, yc = mla_gdn_mixer(h, hc, ab_w_in[i], mla_q_a_norm[i], mla_w_uq[i], mla_kv_a_norm[i], mla_w_ukv[i],
                                  mla_q_norm[i], mla_k_norm[i], gdn_conv_w[i], gdn_a_log[i], gdn_dt_bias[i],
                                  gdn_out_norm[i], ab_w_out[i], need_ctx)
        else:
            lam_init = 0.8 - 0.6 * math.exp(-0.3 * layer)
            y, yc = diff_gla_mixer(h, hc, cd_w_in[i], diff_q_norm[i], diff_k_norm[i], diff_lambda[i],
                                   diff_sub_norm[i], gla_gate_w2[i], gla_gate_b2[i], gla_out_norm[i], cd_w_out[i],
                                   lam_init, need_ctx)
        x = x + gate1 * y
        x = x + gate2 * swiglu(modulate(x, norm_ffn_g[layer], shift2, scale2),
                               ffn_w_gate[layer], ffn_w_up[layer], ffn_w_down[layer])
        if need_ctx:
            ctx = ctx + cgate1 * yc
            ctx = ctx + cgate2 * swiglu(modulate(ctx, norm_ffn_g[layer], cshift2, cscale2),
                                        ffn_w_gate[layer], ffn_w_up[layer], ffn_w_down[layer])
    return x
```

```python
import math
import numpy as np
from contextlib import ExitStack
import concourse.bass as bass
import concourse.mybir as mybir
from concourse.bass_utils import run_bass_kernel_spmd

F32 = mybir.dt.float32
BF16 = mybir.dt.bfloat16
ALU = mybir.AluOpType
AF = mybir.ActivationFunctionType
AX = mybir.AxisListType

D = 2048
KC = 16
NMOD = 6
EPS = 1e-6
FFN_H = 5632
AB_IN = 5216
CD_IN = 6176
NDMASEM = 12


class Buf:
    __slots__ = ("name", "w", "r", "t")

    def __init__(self, name, t=None):
        self.name = name
        self.w = None
        self.r = []
        self.t = t

    def __getitem__(self, idx):
        return self.t[idx]


class Prog:
    ENG = ("pe", "dve", "act", "pool", "sp")

    def __init__(self, nc, es):
        self.nc = nc
        self.sem = {}
        self.cnt = {}
        for e in ("pe", "dve", "act", "pool"):
            self.sem[e] = es.enter_context(nc.semaphore("s_" + e))
            self.cnt[e] = 0
        self.dsem = {}
        for q in ("sp", "pool", "act"):
            for i in range(NDMASEM):
                k = "d_%s_%d" % (q, i)
                self.sem[k] = es.enter_context(nc.semaphore(k))
                self.cnt[k] = 0
            self.dsem[q] = 0
        self.known = {e: {} for e in self.ENG}
        self.lists = {e: [] for e in self.ENG}
        self.ninst = 0

    def _wait(self, E, key, val):
        if val <= self.known[E].get(key, 0):
            return
        self.known[E][key] = val
        sem = self.sem[key]
        self.lists[E].append(lambda eng, sem=sem, val=val: eng.wait_ge(sem, val))

    def _deps(self, E, reads, writes, is_dma):
        for b in reads:
            if b.w is not None:
                k, v, src = b.w
                if src == E and E == "pe" and not is_dma:
                    continue
                self._wait(E, k, v)
        for b in writes:
            if b.w is not None:
                k, v, src = b.w
                if not (src == E and E == "pe" and not is_dma):
                    self._wait(E, k, v)
            for (k, v, src) in b.r:
                if src == E and not is_dma and not k.startswith("d_"):
                    continue
                self._wait(E, k, v)

    def op(self, E, fn, reads=(), writes=()):
        self._deps(E, reads, writes, False)
        self.cnt[E] += 1
        v = self.cnt[E]
        sem = self.sem[E]
        self.lists[E].append(lambda eng, fn=fn, sem=sem: fn(eng).then_inc(sem, 1))
        ev = (E, v, E)
        for b in reads:
            b.r.append(ev)
        for b in writes:
            b.w = ev
            b.r = []
        self.ninst += 1

    def dma(self, Q, out_ap, in_ap, reads=(), writes=()):
        self._deps(Q, reads, writes, True)
        i = self.dsem[Q]
        self.dsem[Q] = (i + 1) % NDMASEM
        key = "d_%s_%d" % (Q, i)
        self._wait(Q, key, self.cnt[key])
        self.cnt[key] += 16
        v = self.cnt[key]
        sem = self.sem[key]
        self.lists[Q].append(lambda eng, o=out_ap, a=in_ap, sem=sem: eng.dma_start(out=o, in_=a).then_inc(sem, 16))
        ev = (key, v, Q)
        for b in reads:
            b.r.append(ev)
        for b in writes:
            b.w = ev
            b.r = []
        self.ninst += 1

    def drain_all(self):
        for E in self.ENG:
            for k in list(self.sem.keys()):
                if self.cnt[k] > 0:
                    self._wait(E, k, self.cnt[k])

    def flush(self):
        nc = self.nc
        lists = self.lists
        with nc.Block() as block:
            @block.tensor
            def _(e):
                for f in lists["pe"]:
                    f(e)

            @block.vector
            def _(e):
                for f in lists["dve"]:
                    f(e)

            @block.scalar
            def _(e):
                for f in lists["act"]:
                    f(e)

            @block.gpsimd
            def _(e):
                for f in lists["pool"]:
                    f(e)

            @block.sync
            def _(e):
                for f in lists["sp"]:
                    f(e)
        self.lists = {e: [] for e in self.ENG}


class Phase:
    def __init__(self, K, name):
        self.K = K
        self.name = name
        self.es = ExitStack()
        self.n = 0

    def __enter__(self):
        self.es.__enter__()
        return self

    def __exit__(self, *a):
        if a[0] is None:
            self.K.P.drain_all()
            self.K.P.flush()
        return self.es.__exit__(*a)

    def sb(self, name, shape, dt=F32):
        self.n += 1
        nm = "%s_%s_%d" % (self.name, name, self.n)
        return Buf(nm, self.es.enter_context(self.K.nc.sbuf_tensor(nm, shape, dt)))

    def pool(self, name, shape, dt=F32, bufs=2):
        return Rot([self.sb(name, shape, dt) for _ in range(bufs)])

    def psum(self, name, shape, dt=F32, bufs=1):
        out = []
        for _ in range(bufs):
            self.n += 1
            nm = "%s_%s_%d" % (self.name, name, self.n)
            out.append(Buf(nm, self.es.enter_context(self.K.nc.psum_tensor(nm, shape, dt))))
        return Rot(out)


class Rot:
    def __init__(self, bufs):
        self.bufs = bufs
        self.i = 0

    def get(self):
        b = self.bufs[self.i]
        self.i = (self.i + 1) % len(self.bufs)
        return b


class Kern:
    def __init__(self, TL, TC, debug=(), layers=(0, 1)):
        self.TL, self.TC = TL, TC
        self.T = TL + TC
        self.NT = self.T // 128
        self.NTC = TC // 128
        self.debug = set(debug)
        self.layers = layers
        self.nc = bass.Bass("TRN2", target_bir_lowering=False)
        self.es = ExitStack()
        self.P = Prog(self.nc, self.es)
        self.aps = {}
        self.dbuf = {}
        self.in_names = []
        self.out_names = []

    def din(self, name, shape, dt=F32):
        self.aps[name] = self.nc.dram_tensor(name, list(shape), dt, kind="ExternalInput").ap()
        self.dbuf[name] = Buf(name)
        self.in_names.append(name)
        return self.aps[name]

    def dscr(self, name, shape, dt=F32, out=False):
        kind = "ExternalOutput" if (out or name in self.debug) else "Internal"
        self.aps[name] = self.nc.dram_tensor(name, list(shape), dt, kind=kind).ap()
        self.dbuf[name] = Buf(name)
        if kind == "ExternalOutput":
            self.out_names.append(name)
        return self.aps[name]

    def mm(self, out, lhsT, rhs, start, stop, R, W):
        self.P.op("pe", lambda e: e.matmul(out, lhsT=lhsT, rhs=rhs, start=start, stop=stop), R, W)

    def tr(self, out, in_, ident, R, W):
        self.P.op("pe", lambda e: e.transpose(out, in_, ident), R, W)

    def act(self, out, in_, func, R, W, scale=1.0, bias=0.0, accum=None):
        if accum is None:
            self.P.op("act", lambda e: e.activation(out=out, in_=in_, func=func, scale=scale, bias=bias), R, W)
        else:
            self.P.op("act", lambda e: e.activation(out=out, in_=in_, func=func, scale=scale, bias=bias,
                                                    accum_out=accum), R, W)

    def ts(self, E, out, in0, s1, s2, op0, op1, R, W):
        E = "dve"
        if s2 is None:
            self.P.op(E, lambda e: e.tensor_scalar(out=out, in0=in0, scalar1=s1, scalar2=None, op0=op0), R, W)
        else:
            self.P.op(E, lambda e: e.tensor_scalar(out=out, in0=in0, scalar1=s1, scalar2=s2, op0=op0, op1=op1), R, W)

    def tt(self, E, out, in0, in1, op, R, W):
        self.P.op(E, lambda e: e.tensor_tensor(out=out, in0=in0, in1=in1, op=op), R, W)

    def stt(self, E, out, in0, scalar, in1, op0, op1, R, W):
        E = "dve"
        self.P.op(E, lambda e: e.scalar_tensor_tensor(out=out, in0=in0, scalar=scalar, in1=in1, op0=op0, op1=op1), R, W)

    def cp(self, E, out, in_, R, W):
        if E == "act":
            self.P.op("act", lambda e: e.copy(out=out, in_=in_), R, W)
        else:
            self.P.op(E, lambda e: e.tensor_copy(out=out, in_=in_), R, W)

    def red(self, out, in_, R, W, op=ALU.add):
        self.P.op("dve", lambda e: e.tensor_reduce(out=out, in_=in_, axis=AX.X, op=op), R, W)

    def recip(self, out, in_, R, W):
        self.P.op("dve", lambda e: e.reciprocal(out=out, in_=in_), R, W)

    def memset(self, E, out, val, W):
        self.P.op(E, lambda e: e.memset(out, val), (), W)

    def ld(self, out, in_, R, W, q="sp"):
        self.P.dma(q, out, in_, R, W)

    def st(self, out, in_, R, W, q="pool"):
        self.P.dma(q, out, in_, R, W)

    def rstd(self, out, ss, n, R, W):
        self.act(out, ss, AF.Ln, R, W, scale=1.0 / n, bias=self.epsc[:, 0:1])
        self.act(out, out, AF.Exp, W, W, scale=-0.5)

    def declare(self):
        TL, TC, T = self.TL, self.TC, self.T
        d = self.din
        d("x", [TL, D]); d("ctx", [TC, D])
        d("cT", [128, 2, KC])
        d("ada_w", [2, D, NMOD * D]); d("ada_bT", [2, 128, NMOD * KC])
        d("gmixT", [2, 128, KC]); d("gffnT", [2, 128, KC])
        d("ffn_w_gate", [2, D, FFN_H]); d("ffn_w_up", [2, D, FFN_H]); d("ffn_w_down", [2, FFN_H, D])
        d("ab_w_in", [D, AB_IN]); d("mla_w_uq", [512, 1536]); d("mla_w_ukv", [512, 2048]); d("ab_w_out", [D, D])
        d("cd_w_in", [D, CD_IN]); d("cd_w_out", [D, D])
        d("qa_g", [128, 512]); d("kva_g", [128, 512]); d("q_g", [128, 192]); d("k_g", [128, 192])
        d("gdn_on", [128, 128]); d("gdn_alog", [128, 16]); d("gdn_dtb", [128, 16]); d("convT", [128, 24, 5])
        d("dq_g", [128, 128]); d("dk_g", [128, 128]); d("dsub_g", [128, 128]); d("gla_on", [128, 256])
        d("dlam", [1, 256]); d("gw2", [2, 33, 512])
        d("ropeC", [TL, 64]); d("ropeS", [TL, 64])
        d("ident", [128, 128]); d("masks", [6, 128, 128])
        s = self.dscr
        s("xs", [T, D]); s("xb", [T, D])
        s("out", [TL, D], out=True)
        for l in (0, 1):
            s("adawb%d" % l, [D, NMOD * D], BF16)
            s("wg%d" % l, [D, FFN_H], BF16); s("wu%d" % l, [D, FFN_H], BF16); s("wd%d" % l, [FFN_H, D], BF16)
        s("abin", [D, AB_IN], BF16); s("uq", [512, 1536], BF16); s("ukv", [512, 2048], BF16); s("about", [D, D], BF16)
        s("cdin", [D, CD_IN], BF16); s("cdout", [D, D], BF16)
        s("hT", [128, KC, T], BF16)
        s("qT", [16, 128, T], BF16)
        s("qTr", [8, 64, T], BF16)
        s("kT", [16, 128, T], BF16)
        s("kTr", [8, 64, T], BF16)
        s("vv", [T, 8, 130], BF16)
        s("mix", [T, D], BF16)
        s("attf", [T, 16, 128], F32)
        s("gqkv", [24, 128, T], F32)
        s("gcv", [24, 128, T], F32)
        s("zs", [T, 1024], BF16)
        s("gb", [T, 32], F32)
        s("rec", [2, T, 1024], F32)
        s("glq", [4, 128, T], F32); s("glk", [4, 128, T], F32)
        s("glkt", [T, 512], F32); s("glv", [T, 1024], F32)
        s("gllog", [2, T, 512], F32)
        s("glr", [32, T], F32)
        s("h2T", [128, KC, T], BF16)
        s("gbcd", [2, 2, 128, D])

    def R(self, name):
        return self.dbuf[name]

    def A(self, name):
        return self.aps[name]

    def consts(self):
        nc, es = self.nc, self.es

        def sb(name, shape, dt=F32):
            return Buf(name, es.enter_context(nc.sbuf_tensor(name, shape, dt)))
        self.ident = sb("identf", [128, 128])
        self.identb = sb("identb", [128, 128], BF16)
        self.ones = sb("ones", [128, 128])
        self.onesb = sb("onesb", [128, 128], BF16)
        self.epsc = sb("epsc", [128, 1])
        self.masks = sb("masksb", [128, 6, 128])
        self.mod = sb("mod", [128, 2, NMOD * KC])
        self.gm = sb("gm", [128, 2, 2, KC])
        self.cT = sb("cTs", [128, 2, KC])
        self.scb = sb("scb", [128, 2, KC], BF16)

    def phase_setup(self):
        with Phase(self, "S") as ph:
            self.ld(self.ident[:], self.A("ident"), [], [self.ident])
            self.ld(self.masks[:], self.A("masks").rearrange("m p n -> p m n"), [], [self.masks])
            self.ld(self.cT[:], self.A("cT"), [], [self.cT])
            self.cp("dve", self.identb[:], self.ident[:], [self.ident], [self.identb])
            self.memset("dve", self.ones[:], 1.0, [self.ones])
            self.memset("dve", self.onesb[:], 1.0, [self.onesb])
            self.memset("dve", self.epsc[:], EPS, [self.epsc])
            self.act(self.scb[:], self.cT[:], AF.Silu, [self.cT], [self.scb])
            self.ld(self.A("xs")[0:self.TC, :], self.A("ctx"), [], [self.R("xs")])
            self.ld(self.A("xs")[self.TC:self.T, :], self.A("x"), [], [self.R("xs")])
            stg = ph.pool("stg", [128, 2048], F32, 3)
            stb = ph.pool("stb", [128, 2048], BF16, 3)
            jobs = []
            for l in self.layers:
                jobs.append((self.A("ada_w")[l], "adawb%d" % l, D, NMOD * D))
                jobs.append((self.A("ffn_w_gate")[l], "wg%d" % l, D, FFN_H))
                jobs.append((self.A("ffn_w_up")[l], "wu%d" % l, D, FFN_H))
                jobs.append((self.A("ffn_w_down")[l], "wd%d" % l, FFN_H, D))
            if 0 in self.layers:
                jobs += [(self.A("ab_w_in"), "abin", D, AB_IN), (self.A("mla_w_uq"), "uq", 512, 1536),
                         (self.A("mla_w_ukv"), "ukv", 512, 2048), (self.A("ab_w_out"), "about", D, D)]
            if 1 in self.layers:
                jobs += [(self.A("cd_w_in"), "cdin", D, CD_IN), (self.A("cd_w_out"), "cdout", D, D)]
            i = 0
            engs = ("dve", "act", "pool")
            for (src, dn, K_, N_) in jobs:
                dst = self.A(dn)
                for k in range(K_ // 128):
                    for c0 in range(0, N_, 2048):
                        w = min(2048, N_ - c0)
                        a = stg.get(); b = stb.get()
                        self.ld(a[:, 0:w], src[k * 128:(k + 1) * 128, c0:c0 + w], [], [a])
                        self.cp(engs[i % 3], b[:, 0:w], a[:, 0:w], [a], [b])
                        self.st(dst[k * 128:(k + 1) * 128, c0:c0 + w], b[:, 0:w], [b], [self.R(dn)], q="act" if i % 2 else "pool")
                        i += 1

    def phase_mod(self, l):
        with Phase(self, "M%d" % l) as ph:
            wp = ph.pool("w", [128, KC, 512], BF16, 3)
            ps = ph.psum("ps", [128, 512], F32, 1).get()
            adab = ph.sb("adab", [128, NMOD * KC])
            gmix = ph.sb("gmix", [128, KC]); gffn = ph.sb("gffn", [128, KC])
            self.ld(adab[:], self.A("ada_bT")[l], [], [adab])
            self.ld(gmix[:], self.A("gmixT")[l], [], [gmix])
            self.ld(gffn[:], self.A("gffnT")[l], [], [gffn])
            wsrc = self.A("adawb%d" % l).rearrange("(k p) n -> p k n", p=128)
            for cb in range(NMOD * D // 512):
                w = wp.get()
                self.ld(w[:], wsrc[:, :, cb * 512:(cb + 1) * 512], [self.R("adawb%d" % l)], [w])
                for jj in range(4):
                    j = cb * 4 + jj
                    for k in range(KC):
                        self.mm(ps[:, 2 * j:2 * j + 2], w[:, k, jj * 128:(jj + 1) * 128], self.scb[:, :, k],
                                k == 0, k == KC - 1, [w, self.scb], [ps])
            psv = ps[:, 0:2 * NMOD * KC].rearrange("p (j s) -> p s j", s=2)
            for s in range(2):
                self.tt("dve", self.mod[:, s, :], psv[:, s, :], adab[:], ALU.add, [ps, adab], [self.mod])
            for s in range(2):
                self.stt("dve", self.gm[:, 0, s, :], self.mod[:, s, 1 * KC:2 * KC], 1.0, gmix[:], ALU.add, ALU.mult,
                         [self.mod, gmix], [self.gm])
                self.stt("dve", self.gm[:, 1, s, :], self.mod[:, s, 4 * KC:5 * KC], 1.0, gffn[:], ALU.add, ALU.mult,
                         [self.mod, gffn], [self.gm])
            dg = ph.pool("dg", [128, 512], F32, 2)
            gbo = ph.pool("gbo", [128, 512], F32, 2)
            pb = ph.psum("pb", [128, 512], F32, 2)
            for which, mi in ((0, 2), (1, 5)):
                for s in range(2):
                    for c4 in range(KC // 4):
                        dgt = dg.get()
                        for q in range(4):
                            c = c4 * 4 + q
                            self.ts("dve", dgt[:, q * 128:(q + 1) * 128], self.ident[:],
                                    self.mod[:, s, mi * KC + c:mi * KC + c + 1], None, ALU.mult, None,
                                    [self.ident, self.mod], [dgt])
                        p = pb.get()
                        self.mm(p[:], self.ones[:], dgt[:], True, True, [self.ones, dgt], [p])
                        go = gbo.get()
                        self.cp("act", go[:], p[:], [p], [go])
                        self.st(self.A("gbcd")[which, s][:, c4 * 512:(c4 + 1) * 512], go[:], [go], [self.R("gbcd")])

    def phase_norm(self, which, dstname, l):
        with Phase(self, "N%d%d" % (which, l)) as ph:
            xp = ph.pool("x", [128, D], F32, 3)
            jk = ph.pool("jk", [128, D], BF16, 2)
            xn = ph.pool("xn", [128, D], BF16, 2)
            st_ = ph.pool("st", [128, 2], F32, 4)
            pt = ph.psum("pt", [128, D], BF16, 2)
            hp = ph.pool("h", [128, KC, 128], BF16, 3)
            sh_i = 0 if which == 0 else 3
            for n in range(self.NT):
                s = 1 if n < self.NTC else 0
                x = xp.get(); j = jk.get(); xx = xn.get(); stt_ = st_.get(); p = pt.get(); h = hp.get()
                self.ld(x[:], self.A("xs")[n * 128:(n + 1) * 128, :], [self.R("xs")], [x])
                self.act(j[:], x[:], AF.Square, [x], [j, stt_], accum=stt_[:, 0:1])
                self.rstd(stt_[:, 1:2], stt_[:, 0:1], D, [stt_], [stt_])
                self.ts("dve", xx[:], x[:], stt_[:, 1:2], None, ALU.mult, None, [x, stt_], [xx])
                for c in range(KC):
                    self.tr(p[:, c * 128:(c + 1) * 128], xx[:, c * 128:(c + 1) * 128], self.identb[:], [xx, self.identb], [p])
                for c in range(KC):
                    gcol = self.gm[:, which, s, c:c + 1]
                    scol = self.mod[:, s, sh_i * KC + c:sh_i * KC + c + 1]
                    if c % 2 == 0:
                        self.act(h[:, c, :], p[:, c * 128:(c + 1) * 128], AF.Identity, [p, self.gm, self.mod], [h],
                                 scale=gcol, bias=scol)
                    else:
                        self.ts("dve", h[:, c, :], p[:, c * 128:(c + 1) * 128], gcol, scol, ALU.mult, ALU.add,
                                [p, self.gm, self.mod], [h])
                self.st(self.A(dstname)[:, :, n * 128:(n + 1) * 128], h[:], [h], [self.R(dstname)])

    def blocks(self, bs=512):
        out = []
        t = 0
        while t < self.TC:
            n = min(bs, self.TC - t); out.append((t, n, True)); t += n
        while t < self.T:
            n = min(bs, self.T - t); out.append((t, n, False)); t += n
        return out

    @staticmethod
    def bcast(ap, axis, n):
        a = ap.unsqueeze(axis)
        shp = list(a.shape)
        shp[axis] = n
        return a.broadcast_to(shp)

    def head_norm_rope(self, ph, pre, nh, hd, gbc, rope, cs, pools, tag, gsel=None):
        jk, ssp, outp, tmpp = pools
        if gsel is None:
            gsel = lambda h: gbc[:, 0:hd]
        j = jk.get(); ss = ssp.get(); o = outp.get()
        self.act(j[:, 0:nh, 0:hd], pre[:, 0:nh, 0:hd], AF.Square, [pre], [j])
        self.red(ss[:, 0:nh], j[:, 0:nh, 0:hd], [j], [ss])
        self.rstd(ss[:, 16:16 + nh], ss[:, 0:nh], hd, [ss], [ss])
        if rope:
            f = tmpp.get()
            for h in range(nh):
                self.stt("dve", f[:, h, 0:hd], pre[:, h, 0:hd], ss[:, 16 + h:17 + h], gsel(h),
                         ALU.mult, ALU.mult, [pre, ss, gbc], [f])
            C, S = cs
            r0 = hd - 64
            if r0 > 0:
                self.cp("act", o[:, 0:nh, 0:r0], f[:, 0:nh, 0:r0], [f], [o])
            t1 = tmpp.get()
            xr = f[:, 0:nh, r0:hd]
            self.tt("dve", t1[:, 0:nh, 0:64], xr, self.bcast(C[:, 0:64], 1, nh), ALU.mult, [f, C], [t1])
            xv = f[:, 0:nh, r0:hd].rearrange("p h (a b c) -> p h a b c", a=2, b=2)
            tv = t1[:, 0:nh, 64:128].rearrange("p h (a b c) -> p h a b c", a=2, b=2)
            Sv = S[:, 0:64].rearrange("p (a b c) -> p a b c", a=2, b=2)
            for a in range(2):
                for b in range(2):
                    self.tt("pool", tv[:, :, a, b, :], xv[:, :, a, 1 - b, :], self.bcast(Sv[:, a, b, :], 1, nh), ALU.mult,
                            [f, S], [t1])
            self.tt("dve", o[:, 0:nh, r0:hd], t1[:, 0:nh, 0:64], t1[:, 0:nh, 64:128], ALU.add, [t1], [o])
        else:
            for h in range(nh):
                self.stt("dve", o[:, h, 0:hd], pre[:, h, 0:hd], ss[:, 16 + h:17 + h], gsel(h),
                         ALU.mult, ALU.mult, [pre, ss, gbc], [o])
        return o

    def phase_inproj0(self):
        with Phase(self, "A0") as ph:
            TC = self.TC
            wsrc = self.A("abin").rearrange("(k p) n -> p k n", p=128)
            RW = [self.R("abin")]
            w_q = ph.sb("w_q", [128, KC, 1088], BF16)
            w_ab = ph.sb("w_ab", [128, KC, 32], BF16)
            wzp = ph.pool("wz", [128, KC, 512], BF16, 1)
            self.ld(w_q[:], wsrc[:, :, 0:1088], RW, [w_q])
            self.ld(w_ab[:], wsrc[:, :, 5184:5216], RW, [w_ab])
            uq = ph.sb("uq", [128, 4, 1536], BF16); ukv = ph.sb("ukv", [128, 4, 2048], BF16)
            self.ld(uq[:], self.A("uq").rearrange("(k p) n -> p k n", p=128), [self.R("uq")], [uq])
            self.ld(ukv[:], self.A("ukv").rearrange("(k p) n -> p k n", p=128), [self.R("ukv")], [ukv])
            cst = {}
            for nm, w in (("qa_g", 512), ("kva_g", 512), ("q_g", 192), ("k_g", 192), ("gdn_alog", 16), ("gdn_dtb", 16)):
                cst[nm] = ph.sb(nm, [128, w])
                self.ld(cst[nm][:], self.A(nm), [], [cst[nm]])
            ealog = ph.sb("ealog", [128, 16])
            self.act(ealog[:], cst["gdn_alog"][:], AF.Exp, [cst["gdn_alog"]], [ealog])
            hbp = ph.pool("hb", [128, KC, 512], BF16, 1)
            wfp = ph.pool("wf", [128, KC, 128], BF16, 2)
            psA = ph.psum("psA", [128, 512], F32, 4)
            psT = ph.psum("psT", [128, 1024], BF16, 2)
            fo = ph.pool("fo", [128, 512], F32, 2)
            jk5 = ph.pool("jk5", [128, 512], BF16, 2)
            st4 = ph.pool("st4", [128, 4], F32, 4)
            anb = ph.pool("anb", [128, 512], BF16, 2)
            anT = ph.pool("anT", [128, 4, 128], BF16, 4)
            krp = ph.pool("kr", [128, 64], F32, 2)
            qpre = ph.pool("qpre", [128, 8, 192], F32, 1)
            kvpre = ph.pool("kvpre", [128, 8, 256], F32, 1)
            kpre = ph.pool("kpre", [128, 8, 192], F32, 1)
            hpools = (ph.pool("hj", [128, 8, 192], BF16, 1), ph.pool("hss", [128, 32], F32, 4),
                      ph.pool("ho", [128, 8, 192], BF16, 2), ph.pool("ht", [128, 8, 192], F32, 2))
            vvp = ph.pool("vv", [128, 8, 130], BF16, 2)
            tn = ph.pool("tn", [128, 8, 128], BF16, 2)
            trp = ph.pool("trp", [64, 8, 128], BF16, 2)
            zp = ph.pool("zp", [128, 512], BF16, 2)
            gbp = ph.pool("gbp", [128, 32], F32, 2)
            g1 = ph.pool("g1", [128, 16], F32, 4)
            rc = ph.pool("rc", [128, 64], F32, 2); rs = ph.pool("rs", [128, 64], F32, 2)
            ev = 0
            for (t0, ntok, isctx) in self.blocks(512):
                hb = hbp.get()
                self.ld(hb[:, :, 0:ntok], self.A("hT")[:, :, t0:t0 + ntok], [self.R("hT")], [hb])
                for ch in range(24):
                    w = wfp.get()
                    self.ld(w[:], wsrc[:, :, 1088 + ch * 128:1088 + (ch + 1) * 128], RW, [w])
                    p = psA.get()
                    for k in range(KC):
                        self.mm(p[:, 0:ntok], w[:, k, :], hb[:, k, 0:ntok], k == 0, k == KC - 1, [w, hb], [p])
                    f = fo.get()
                    self.cp("act" if ch % 2 else "dve", f[:, 0:ntok], p[:, 0:ntok], [p], [f])
                    self.st(self.A("gqkv")[ch][:, t0:t0 + ntok], f[:, 0:ntok], [f], [self.R("gqkv")])
                for cb in range(2):
                    wz = wzp.get()
                    self.ld(wz[:], wsrc[:, :, 4160 + cb * 512:4160 + (cb + 1) * 512], RW, [wz])
                    for ti in range(ntok // 128):
                        p = psA.get()
                        for k in range(KC):
                            self.mm(p[:, 0:512], hb[:, k, ti * 128:(ti + 1) * 128], wz[:, k, :], k == 0, k == KC - 1, [hb, wz], [p])
                        z = zp.get()
                        self.act(z[:, 0:512], p[:, 0:512], AF.Silu, [p], [z])
                        self.st(self.A("zs")[t0 + ti * 128:t0 + (ti + 1) * 128, cb * 512:(cb + 1) * 512], z[:, 0:512], [z], [self.R("zs")])
                for ti in range(ntok // 128):
                    tk = t0 + ti * 128
                    sl = slice(ti * 128, (ti + 1) * 128)

                    def lin(wt, c0, wdt, kc=KC, src=hb, ssl=sl):
                        p = psA.get()
                        for k in range(kc):
                            self.mm(p[:, 0:wdt], src[:, k, ssl], wt[:, k, c0:c0 + wdt], k == 0, k == kc - 1, [src, wt], [p])
                        return p

                    def norm_T(p, gname):
                        j = jk5.get(); s4 = st4.get(); a = anb.get(); aT = anT.get()
                        self.act(j[:], p[:, 0:512], AF.Square, [p], [j, s4], accum=s4[:, 0:1])
                        self.rstd(s4[:, 1:2], s4[:, 0:1], 512, [s4], [s4])
                        self.stt("dve", a[:], p[:, 0:512], s4[:, 1:2], cst[gname][:], ALU.mult, ALU.mult, [p, s4, cst[gname]], [a])
                        pt = psT.get()
                        for c in range(4):
                            self.tr(pt[:, c * 128:(c + 1) * 128], a[:, c * 128:(c + 1) * 128], self.identb[:], [a, self.identb], [pt])
                        self.cp("act", aT[:].rearrange("p c t -> p (c t)"), pt[:, 0:512], [pt], [aT])
                        return aT
                    qaT = norm_T(lin(w_q, 0, 512), "qa_g")
                    kvaT = norm_T(lin(w_q, 512, 512), "kva_g")
                    pk = lin(w_q, 1024, 64)
                    kr = krp.get()
                    self.cp("dve", kr[:], pk[:, 0:64], [pk], [kr])
                    qp = qpre.get()
                    for cb in range(3):
                        p = lin(uq, cb * 512, 512, 4, qaT, slice(0, 128))
                        self.cp("act" if cb % 2 else "dve", qp[:].rearrange("p h d -> p (h d)")[:, cb * 512:(cb + 1) * 512], p[:, 0:512], [p], [qp])
                    kvp = kvpre.get()
                    for cb in range(4):
                        p = lin(ukv, cb * 512, 512, 4, kvaT, slice(0, 128))
                        self.cp("act" if cb % 2 else "dve", kvp[:].rearrange("p h d -> p (h d)")[:, cb * 512:(cb + 1) * 512], p[:, 0:512], [p], [kvp])
                    kp = kpre.get()
                    self.cp("pool", kp[:, :, 0:128], kvp[:, :, 0:128], [kvp], [kp])
                    self.cp("pool", kp[:, :, 128:192], self.bcast(kr[:, 0:64], 1, 8), [kr], [kp])
                    v = vvp.get()
                    self.cp("act", v[:, :, 0:128], kvp[:, :, 128:256], [kvp], [v])
                    self.memset("pool", v[:, :, 128:130], 1.0, [v])
                    self.st(self.A("vv")[tk:tk + 128], v[:], [v], [self.R("vv")])
                    cs = None
                    if not isctx:
                        C = rc.get(); S = rs.get()
                        self.ld(C[:], self.A("ropeC")[tk - TC:tk - TC + 128, :], [], [C])
                        self.ld(S[:], self.A("ropeS")[tk - TC:tk - TC + 128, :], [], [S])
                        cs = (C, S)
                    for (pre, gname, dn, dr) in ((qp, "q_g", "qT", "qTr"), (kp, "k_g", "kT", "kTr")):
                        o = self.head_norm_rope(ph, pre, 8, 192, cst[gname], not isctx, cs, hpools, dn)
                        pt = psT.get()
                        for h in range(8):
                            self.tr(pt[:, h * 128:(h + 1) * 128], o[:, h, 0:128], self.identb[:], [o, self.identb], [pt])
                        a = tn.get()
                        self.cp("act", a[:].rearrange("p h t -> p (h t)"), pt[:, 0:1024], [pt], [a])
                        self.st(self.A(dn)[0:8].rearrange("g p t -> p g t")[:, :, tk:tk + 128], a[:], [a], [self.R(dn)])
                        pt2 = psT.get()
                        for h in range(8):
                            self.tr(pt2[0:64, h * 128:(h + 1) * 128], o[:, h, 128:192], self.identb[:], [o, self.identb], [pt2])
                        a2 = trp.get()
                        self.cp("dve", a2[:].rearrange("p h t -> p (h t)"), pt2[0:64, 0:1024], [pt2], [a2])
                        self.st(self.A(dr).rearrange("g p t -> p g t")[:, :, tk:tk + 128], a2[:], [a2], [self.R(dr)])
                    p = lin(w_ab, 0, 32)
                    gb = gbp.get(); t1 = g1.get(); t2 = g1.get()
                    self.tt("dve", t1[:], p[:, 0:16], cst["gdn_dtb"][:], ALU.add, [p, cst["gdn_dtb"]], [t1])
                    self.act(t2[:], t1[:], AF.Exp, [t1], [t2])
                    self.act(t2[:], t2[:], AF.Ln, [t2], [t2], bias=1.0)
                    self.stt("dve", gb[:, 0:16], t2[:], -1.0, ealog[:], ALU.mult, ALU.mult, [t2, ealog], [gb])
                    t3 = g1.get()
                    self.act(t3[:], p[:, 16:32], AF.Exp, [p], [t3], scale=-1.0)
                    self.ts("dve", t3[:], t3[:], 1.0, None, ALU.add, None, [t3], [t3])
                    self.recip(gb[:, 16:32], t3[:], [t3], [gb])
                    self.st(self.A("gb")[tk:tk + 128, :], gb[:], [gb], [self.R("gb")])

    def phase_attn(self, name, G, dk_main, has_rope_part, vmap, scale, with_ctx, sink):
        with Phase(self, name) as ph:
            T, TC, NT = self.T, self.TC, self.NT
            ktp = ph.pool("kt", [128, T], BF16, 2)
            krp = ph.pool("ktr", [64, T], BF16, 2) if has_rope_part else None
            vp = ph.pool("v", [128, NT, 130], BF16, 2)
            qp = ph.pool("q", [128, 512], BF16, 2)
            qrp = ph.pool("qr", [64, 512], BF16, 2) if has_rope_part else None
            psS = ph.psum("S", [128, 512], F32, 3)
            psO = [ph.psum("O%d" % i, [128, 512], F32, 1).get() for i in range(4)]
            ptp = ph.pool("pt", [128, 512], BF16, 3)
            rcp = ph.pool("rc", [128, 1], F32, 4)
            op = ph.pool("o", [128, 128], F32, 4)
            qblocks = []
            if with_ctx:
                for (t0, n, c) in self.blocks(512):
                    if c:
                        qblocks.append((t0, n, TC))
            for (t0, n, c) in self.blocks(512):
                if not c:
                    qblocks.append((t0, n, T))
            vsrc = self.A("vv").rearrange("(n p) h c -> p n h c", p=128)
            for g in range(G):
                kt = ktp.get()
                self.ld(kt[0:dk_main, :], self.A("kT")[g][0:dk_main, :], [self.R("kT")], [kt])
                if has_rope_part:
                    kr = krp.get()
                    self.ld(kr[:], self.A("kTr")[g], [self.R("kTr")], [kr])
                v = vp.get()
                self.ld(v[:], vsrc[:, :, vmap(g), :], [self.R("vv")], [v])
                for (q0, nq, kend) in qblocks:
                    q = qp.get()
                    self.ld(q[0:dk_main, 0:nq], self.A("qT")[g][0:dk_main, q0:q0 + nq], [self.R("qT")], [q])
                    if has_rope_part:
                        qr = qrp.get()
                        self.ld(qr[:, 0:nq], self.A("qTr")[g][:, q0:q0 + nq], [self.R("qTr")], [qr])
                    nkt = kend // 128
                    for kti in range(nkt):
                        ks = slice(kti * 128, (kti + 1) * 128)
                        S = psS.get()
                        self.mm(S[:, 0:nq], kt[0:dk_main, ks], q[0:dk_main, 0:nq], True, not has_rope_part, [kt, q], [S])
                        if has_rope_part:
                            self.mm(S[:, 0:nq], kr[:, ks], qr[:, 0:nq], False, True, [kr, qr], [S])
                        pt = ptp.get()
                        self.act(pt[:, 0:nq], S[:, 0:nq], AF.Exp, [S], [pt], scale=scale)
                        for qs in range(nq // 128):
                            self.mm(psO[qs][:, 0:130], pt[:, qs * 128:(qs + 1) * 128], v[:, kti, :], kti == 0, kti == nkt - 1,
                                    [pt, v], [psO[qs]])
                    for qs in range(nq // 128):
                        r = rcp.get(); o = op.get()
                        self.recip(r[:], psO[qs][:, 128:129], [psO[qs]], [r])
                        self.ts("dve", o[:], psO[qs][:, 0:128], r[:, 0:1], None, ALU.mult, None, [psO[qs], r], [o])
                        sink(ph, g, q0 + qs * 128, o)

    def sink_mix(self, col0):
        cache = {}

        def sink(ph, g, tk, o):
            if ph not in cache:
                cache[ph] = ph.pool("sinkb", [128, 128], BF16, 4)
            b = cache[ph].get()
            self.cp("pool", b[:], o[:], [o], [b])
            self.st(self.A("mix")[tk:tk + 128, col0 + g * 128:col0 + (g + 1) * 128], b[:], [b], [self.R("mix")], q="act")
        return sink

    def sink_attf(self):
        def sink(ph, g, tk, o):
            self.st(self.A("attf")[tk:tk + 128, g, :], o[:], [o], [self.R("attf")], q="act")
        return sink

    def build(self, stop=None):
        self.declare()
        self.consts()
        self.phase_setup()
        steps = []
        if 0 in self.layers:
            steps += [("M0", lambda: self.phase_mod(0)), ("N00", lambda: self.phase_norm(0, "hT", 0)),
                      ("A0", self.phase_inproj0),
                      ("AT0", lambda: self.phase_attn("AT0", 8, 128, True, lambda g: g, 192 ** -0.5, True, self.sink_mix(0))),
                      ("G1", self.phase_gdn_conv), ("G2", self.phase_gdn_scan),
                      ("O0", lambda: self.phase_outproj(0, "xs", "xb", True)),
                      ("F0", lambda: self.phase_ffn(0, "xb", "xs", True, False))]
        if 1 in self.layers:
            steps += [("M1", lambda: self.phase_mod(1)), ("N01", lambda: self.phase_norm(0, "hT", 1)),
                      ("A1", self.phase_inproj1),
                      ("AT1", lambda: self.phase_attn("AT1", 16, 64, False, lambda g: g // 2, 64 ** -0.5, False, self.sink_attf())),
                      ("L2", self.phase_gla_scan),
                      ("O1", lambda: self.phase_outproj(1, "xs", "xb", False)),
                      ("F1", lambda: self.phase_ffn(1, "xb", "xs", False, True))]
        for nm, fn in steps:
            fn()
            if stop == nm:
                break
        self.es.close()
        return self.nc


def rope_tables(TL):
    f32 = np.float32
    rows = TL // 64
    row = np.repeat(np.arange(rows, dtype=f32), 64)
    col = np.tile(np.arange(64, dtype=f32), rows)
    inv = (f32(10000.0) ** (-np.arange(0, 32, 2, dtype=f32) / f32(32))).astype(f32)
    ar = (row[:, None] * inv[None, :]).astype(f32)
    ac = (col[:, None] * inv[None, :]).astype(f32)
    cr, sr, cc, sc = np.cos(ar), np.sin(ar), np.cos(ac), np.sin(ac)
    C = np.concatenate([cr, cr, cc, cc], axis=1).astype(f32)
    S = np.concatenate([-sr, sr, -sc, sc], axis=1).astype(f32)
    return np.ascontiguousarray(C), np.ascontiguousarray(S)


def const_masks():
    i = np.arange(128)[:, None]
    j = np.arange(128)[None, :]
    m = np.stack([i >= j, i <= j, i > j, i < j, i > j, i < j]).astype(np.float32)
    m[4] *= -1.0
    m[5] *= -1.0
    return np.ascontiguousarray(m)


def prep_core_inputs(inp, b, TL, TC):
    f = np.float32
    c = lambda a: np.ascontiguousarray(a, dtype=f)
    tile = lambda v: c(np.tile(np.asarray(v).reshape(1, -1), (128, 1)))
    fp = lambda v: c(np.asarray(v).reshape(-1, 128).T)
    m = {}
    m["x"] = c(inp["x"][b, :TL]); m["ctx"] = c(inp["ctx"][b, :TC])
    m["cT"] = c(np.stack([fp(inp["c"][b]), fp(inp["c_ctx"])], axis=1))
    m["ada_w"] = c(inp["ada_w"]); m["ada_bT"] = c(np.stack([fp(inp["ada_b"][l]) for l in range(2)]))
    m["gmixT"] = c(np.stack([fp(inp["norm_mix_g"][l]) for l in range(2)]))
    m["gffnT"] = c(np.stack([fp(inp["norm_ffn_g"][l]) for l in range(2)]))
    for k in ("ffn_w_gate", "ffn_w_up", "ffn_w_down"):
        m[k] = c(inp[k])
    m["ab_w_in"] = c(inp["ab_w_in"][0]); m["mla_w_uq"] = c(inp["mla_w_uq"][0]); m["mla_w_ukv"] = c(inp["mla_w_ukv"][0])
    m["ab_w_out"] = c(inp["ab_w_out"][0]); m["cd_w_in"] = c(inp["cd_w_in"][0]); m["cd_w_out"] = c(inp["cd_w_out"][0])
    m["qa_g"] = tile(inp["mla_q_a_norm"][0]); m["kva_g"] = tile(inp["mla_kv_a_norm"][0])
    m["q_g"] = tile(inp["mla_q_norm"][0]); m["k_g"] = tile(inp["mla_k_norm"][0])
    m["gdn_on"] = tile(inp["gdn_out_norm"][0]); m["gdn_alog"] = tile(inp["gdn_a_log"][0]); m["gdn_dtb"] = tile(inp["gdn_dt_bias"][0])
    m["convT"] = c(np.asarray(inp["gdn_conv_w"][0]).T.reshape(24, 128, 5).transpose(1, 0, 2))
    m["dq_g"] = tile(inp["diff_q_norm"][0]); m["dk_g"] = tile(inp["diff_k_norm"][0]); m["dsub_g"] = tile(inp["diff_sub_norm"][0])
    m["gla_on"] = tile(inp["gla_out_norm"][0]); m["dlam"] = c(np.asarray(inp["diff_lambda"][0]).reshape(1, 256))
    gw2 = np.zeros((2, 33, 512), f)
    for d_ in range(2):
        gw2[d_, 16 * d_:16 * d_ + 16] = inp["gla_gate_w2"][0][d_]
        gw2[d_, 32] = inp["gla_gate_b2"][0][d_]
    m["gw2"] = gw2
    m["ropeC"], m["ropeS"] = rope_tables(TL)
    m["ident"] = c(np.eye(128)); m["masks"] = const_masks()
    return m


def phase_gdn_conv(self):
    with Phase(self, "G1") as ph:
        T, TC = self.T, self.TC
        cw = ph.sb("cw", [128, 24, 5])
        self.ld(cw[:], self.A("convT"), [], [cw])
        rp = ph.pool("raw", [128, T], F32, 2)
        ap_ = ph.pool("acc", [128, T], F32, 2)
        yp = ph.pool("y", [128, T], F32, 2)
        sqp = ph.pool("sq", [128, 512], F32, 2)
        rbp = ph.pool("rb", [128, 512], F32, 2)
        pp = ph.psum("pp", [128, 512], F32, 3)
        segs = [(0, TC), (TC, T)]
        for ch in range(24):
            raw = rp.get(); acc = ap_.get(); y = yp.get()
            self.ld(raw[:], self.A("gqkv")[ch], [self.R("gqkv")], [raw])
            for (s0, s1) in segs:
                self.ts("dve", acc[:, s0:s1], raw[:, s0:s1], cw[:, ch, 2:3], None, ALU.mult, None, [raw, cw], [acc])
                for j in (0, 1, 3, 4):
                    sh = j - 2
                    a = max(s0, s0 - sh); b = min(s1, s1 - sh)
                    self.stt("dve", acc[:, a:b], raw[:, a + sh:b + sh], cw[:, ch, j:j + 1], acc[:, a:b], ALU.mult, ALU.add,
                             [raw, cw, acc], [acc])
            self.act(y[:], acc[:], AF.Silu, [acc], [y])
            if ch < 16:
                for c0 in range(0, T, 512):
                    w = min(512, T - c0)
                    sq = sqp.get(); rb = rbp.get(); p = pp.get()
                    self.tt("pool", sq[:, 0:w], y[:, c0:c0 + w], y[:, c0:c0 + w], ALU.mult, [y], [sq])
                    self.mm(p[:, 0:w], self.ones[:], sq[:, 0:w], True, True, [self.ones, sq], [p])
                    self.rstd(rb[:, 0:w], p[:, 0:w], 1.0, [p], [rb])
                    if ch < 8:
                        self.stt("dve", y[:, c0:c0 + w], rb[:, 0:w], 128 ** -0.5, y[:, c0:c0 + w], ALU.mult, ALU.mult, [rb, y], [y])
                    else:
                        self.tt("dve", y[:, c0:c0 + w], rb[:, 0:w], y[:, c0:c0 + w], ALU.mult, [rb, y], [y])
            self.st(self.A("gcv")[ch], y[:], [y], [self.R("gcv")])


def phase_gdn_scan(self):
    with Phase(self, "G2") as ph:
        NT, NTC = self.NT, self.NTC
        M = self.masks
        pp = ph.psum("pp", [128, 512], F32, 7)
        S = [[ph.sb("S%d%d" % (d, h), [128, 128]) for h in range(8)] for d in range(2)]
        for d in range(2):
            for h in range(8):
                self.memset("pool", S[d][h][:], 0.0, [S[d][h]])
        gbp = ph.pool("gb", [128, 32], F32, 2)
        kTp = ph.pool("kT", [128, 8, 128], F32, 2); qTp = ph.pool("qT", [128, 8, 128], F32, 2)
        vTp = ph.pool("vT", [128, 8, 128], F32, 2)
        ktokp = ph.pool("ktok", [128, 8, 128], F32, 2); vtokp = ph.pool("vtok", [128, 8, 128], F32, 2)
        colp = ph.pool("col", [128, 64], F32, 2)
        dgp = ph.pool("dg", [128, 256], F32, 2)
        t1p = ph.pool("t1", [128, 128], F32, 2); t2p = ph.pool("t2", [128, 128], F32, 2)
        Yp = ph.pool("Y", [128, 128], F32, 2); Zp = ph.pool("Z", [128, 128], F32, 2); eRp = ph.pool("eR", [128, 128], F32, 2)
        ndp = ph.pool("nd", [128, 128], F32, 2); tmpp = ph.pool("tmp", [128, 128], F32, 2); ndtp = ph.pool("ndt", [128, 128], F32, 2)
        Fp = ph.pool("F", [128, 128], F32, 2)
        TUp = ph.pool("TU", [128, 256], F32, 3); Np = ph.pool("N", [128, 128], F32, 3)
        Tfp = ph.pool("Tf", [128, 128], F32, 2)
        qkfp = ph.pool("qkf", [128, 128], F32, 2); rup = ph.pool("ru", [128, 128], F32, 2); rwp = ph.pool("rw", [128, 128], F32, 2)
        kep = ph.pool("ke", [128, 128], F32, 2); qdp = ph.pool("qd", [128, 128], F32, 2)
        wTp = ph.pool("wT", [128, 128], F32, 2); up_ = ph.pool("u", [128, 128], F32, 2); vnp = ph.pool("vn", [128, 128], F32, 2)
        outp = ph.pool("out", [128, 1024], F32, 2)
        gsrc = self.A("gcv")
        order = {0: list(range(NT)), 1: list(range(NTC - 1, -1, -1)) + list(range(NT - 1, NTC - 1, -1))}
        for step in range(NT):
            for d in range(2):
                n = order[d][step]
                ts_ = slice(n * 128, (n + 1) * 128)
                gb = gbp.get(); kT = kTp.get(); qT = qTp.get(); vT = vTp.get()
                self.ld(gb[:], self.A("gb")[ts_, :], [self.R("gb")], [gb])
                self.ld(qT[:], gsrc[0:8].rearrange("h p t -> p h t")[:, :, ts_], [self.R("gcv")], [qT])
                self.ld(kT[:], gsrc[8:16].rearrange("h p t -> p h t")[:, :, ts_], [self.R("gcv")], [kT])
                self.ld(vT[:], gsrc[16:24].rearrange("h p t -> p h t")[:, :, ts_], [self.R("gcv")], [vT])
                ktok = ktokp.get(); vtok = vtokp.get()
                for (src, dst) in ((kT, ktok), (vT, vtok)):
                    for h4 in range(2):
                        p = pp.get()
                        for hh in range(4):
                            h = h4 * 4 + hh
                            self.tr(p[:, hh * 128:(hh + 1) * 128], src[:, h, :], self.ident[:], [src, self.ident], [p])
                        self.cp("act", dst[:, h4 * 4:(h4 + 1) * 4, :].rearrange("p h t -> p (h t)"), p[:], [p], [dst])
                col = colp.get()
                p = pp.get()
                tri = M[:, 1, :] if d == 0 else M[:, 0, :]
                g_d = gb[:, d * 8:(d + 1) * 8]
                self.mm(p[:, 0:8], tri, g_d, True, True, [M, gb], [p])
                self.mm(p[:, 8:16], self.ones[:], g_d, True, True, [self.ones, gb], [p])
                self.cp("dve", col[:, 0:16], p[:, 0:16], [p], [col])
                self.act(col[:, 16:24], col[:, 0:8], AF.Exp, [col], [col])
                self.tt("dve", col[:, 24:32], col[:, 8:16], col[:, 0:8], ALU.subtract, [col], [col])
                self.act(col[:, 24:32], col[:, 24:32], AF.Exp, [col], [col])
                self.act(col[:, 32:40], col[:, 8:16], AF.Exp, [col], [col])
                beta_d = gb[:, 16 + d * 8:16 + (d + 1) * 8]
                self.tt("dve", col[:, 40:48], beta_d, col[:, 16:24], ALU.mult, [gb, col], [col])
                self.ts("dve", col[:, 48:56], beta_d, -1.0, None, ALU.mult, None, [gb], [col])
                out = outp.get()
                for h in range(8):
                    gc = col[:, h:h + 1]
                    dg = dgp.get()
                    self.ts("dve", dg[:, 0:128], self.ident[:], gc, None, ALU.mult, None, [self.ident, col], [dg])
                    self.ts("dve", dg[:, 128:256], self.ident[:], beta_d[:, h:h + 1], None, ALU.mult, None, [self.ident, gb], [dg])
                    pR = pp.get()
                    self.mm(pR[:, 0:256], self.ones[:], dg[:], True, True, [self.ones, dg], [pR])
                    t1 = t1p.get(); t2 = t2p.get(); Y = Yp.get(); Z = Zp.get(); eR = eRp.get()
                    self.ts("dve", t1[:], pR[:, 0:128], gc, 0.0, ALU.subtract, ALU.min, [pR, col], [t1])
                    self.ts("dve", t2[:], pR[:, 0:128], gc, 0.0, ALU.subtract, ALU.max, [pR, col], [t2])
                    self.act(Y[:], t1[:], AF.Exp, [t1], [Y])
                    self.act(Z[:], t2[:], AF.Exp, [t2], [Z], scale=-1.0)
                    self.act(eR[:], pR[:, 0:128], AF.Exp, [pR], [eR])
                    nd = ndp.get(); tmp = tmpp.get(); ndt = ndtp.get(); Fm = Fp.get()
                    sa = M[:, 2, :] if d == 0 else M[:, 3, :]
                    nsat = M[:, 5, :] if d == 0 else M[:, 4, :]
                    ib = M[:, 1, :] if d == 0 else M[:, 0, :]
                    self.stt("dve", nd[:], Z[:], col[:, 48 + h:49 + h], sa, ALU.mult, ALU.mult, [Z, col, M], [nd])
                    self.tt("dve", tmp[:], Y[:], pR[:, 128:256], ALU.mult, [Y, pR], [tmp])
                    self.tt("pool", ndt[:], tmp[:], nsat, ALU.mult, [tmp, M], [ndt])
                    self.tt("pool", Fm[:], Y[:], ib, ALU.mult, [Y, M], [Fm])
                    pG = pp.get()
                    self.mm(pG[:, 0:128], kT[:, h, :], kT[:, h, :], True, True, [kT], [pG])
                    self.mm(pG[:, 128:256], kT[:, h, :], qT[:, h, :], True, True, [kT, qT], [pG])
                    TU = TUp.get(); N = Np.get()
                    self.tt("dve", N[:], pG[:, 0:128], nd[:], ALU.mult, [pG, nd], [N])
                    self.tt("dve", TU[:, 128:256], pG[:, 0:128], ndt[:], ALU.mult, [pG, ndt], [TU])
                    self.cp("pool", TU[:, 0:128], self.ident[:], [self.ident], [TU])
                    qkf = qkfp.get()
                    self.tt("dve", qkf[:], pG[:, 128:256], Fm[:], ALU.mult, [pG, Fm], [qkf])
                    ru = rup.get(); rw = rwp.get(); ke = kep.get(); qd = qdp.get()
                    self.act(ru[:], vtok[:, h, :], AF.Identity, [vtok, gb], [ru], scale=beta_d[:, h:h + 1])
                    self.act(rw[:], ktok[:, h, :], AF.Identity, [ktok, col], [rw], scale=col[:, 40 + h:41 + h])
                    self.act(ke[:], ktok[:, h, :], AF.Identity, [ktok, col], [ke], scale=col[:, 24 + h:25 + h])
                    self.tt("pool", qd[:], qT[:, h, :], eR[:], ALU.mult, [qT, eR], [qd])
                    for k in range(6):
                        pA = pp.get(); pB = pp.get()
                        self.mm(pA[:, 0:256], N[:], TU[:], True, True, [N, TU], [pA])
                        self.mm(pB[:, 0:128], TU[:, 128:256], N[:], True, True, [N, TU], [pB])
                        TU2 = TUp.get(); N2 = Np.get()
                        self.tt("dve", TU2[:, 0:128], TU[:, 0:128], pA[:, 0:128], ALU.add, [TU, pA], [TU2])
                        self.cp("act", TU2[:, 128:256], pA[:, 128:256], [pA], [TU2])
                        self.cp("act", N2[:], pB[:, 0:128], [pB], [N2])
                        TU, N = TU2, N2
                    pA = pp.get()
                    self.mm(pA[:, 0:128], N[:], TU[:, 0:128], True, True, [N, TU], [pA])
                    Tf = Tfp.get()
                    self.tt("dve", Tf[:], TU[:, 0:128], pA[:, 0:128], ALU.add, [TU, pA], [Tf])
                    pW = pp.get()
                    self.mm(pW[:, 0:128], rw[:], Tf[:], True, True, [rw, Tf], [pW])
                    self.mm(pW[:, 128:256], Tf[:], ru[:], True, True, [Tf, ru], [pW])
                    wT = wTp.get(); u = up_.get()
                    self.cp("act", wT[:], pW[:, 0:128], [pW], [wT])
                    self.cp("act", u[:], pW[:, 128:256], [pW], [u])
                    St = S[d][h]
                    p1 = pp.get()
                    self.mm(p1[:, 0:128], wT[:], St[:], True, True, [wT, St], [p1])
                    vn = vnp.get()
                    self.tt("dve", vn[:], u[:], p1[:, 0:128], ALU.subtract, [u, p1], [vn])
                    pO = pp.get()
                    self.mm(pO[:, 0:128], qd[:], St[:], True, False, [qd, St], [pO])
                    self.mm(pO[:, 0:128], qkf[:], vn[:], False, True, [qkf, vn], [pO])
                    self.cp("act", out[:, h * 128:(h + 1) * 128], pO[:, 0:128], [pO], [out])
                    p2 = pp.get()
                    self.mm(p2[:, 0:128], ke[:], vn[:], True, True, [ke, vn], [p2])
                    self.stt("dve", St[:], St[:], col[:, 32 + h:33 + h], p2[:, 0:128], ALU.mult, ALU.add, [St, col, p2], [St])
                self.st(self.A("rec")[d][ts_, :], out[:], [out], [self.R("rec")])


Kern.phase_gdn_conv = phase_gdn_conv
Kern.phase_gdn_scan = phase_gdn_scan


def norm_tile(self, ph, x, s, which, pools, dstname, n):
    jk, xn, st_, pt, hp = pools
    sh_i = 0 if which == 0 else 3
    j = jk.get(); xx = xn.get(); stt_ = st_.get(); p = pt.get(); h = hp.get()
    self.act(j[:], x[:], AF.Square, [x], [j, stt_], accum=stt_[:, 0:1])
    self.rstd(stt_[:, 1:2], stt_[:, 0:1], D, [stt_], [stt_])
    self.ts("dve", xx[:], x[:], stt_[:, 1:2], None, ALU.mult, None, [x, stt_], [xx])
    for c in range(KC):
        self.tr(p[:, c * 128:(c + 1) * 128], xx[:, c * 128:(c + 1) * 128], self.identb[:], [xx, self.identb], [p])
    for c in range(KC):
        gcol = self.gm[:, which, s, c:c + 1]
        scol = self.mod[:, s, sh_i * KC + c:sh_i * KC + c + 1]
        if c % 2 == 0:
            self.act(h[:, c, :], p[:, c * 128:(c + 1) * 128], AF.Identity, [p, self.gm, self.mod], [h], scale=gcol, bias=scol)
        else:
            self.ts("dve", h[:, c, :], p[:, c * 128:(c + 1) * 128], gcol, scol, ALU.mult, ALU.add, [p, self.gm, self.mod], [h])
    self.st(self.A(dstname)[:, :, n * 128:(n + 1) * 128], h[:], [h], [self.R(dstname)])


def phase_outproj(self, l, xin, xout, need_ctx):
    with Phase(self, "O%d" % l) as ph:
        NT, NTC = self.NT, self.NTC
        wname = "about" if l == 0 else "cdout"
        wo = ph.sb("wo", [128, KC, D], BF16)
        self.ld(wo[:], self.A(wname).rearrange("(k p) n -> p k n", p=128), [self.R(wname)], [wo])
        nh, dv = (8, 128) if l == 0 else (4, 256)
        gon = ph.sb("gon", [128, dv])
        self.ld(gon[:], self.A("gdn_on" if l == 0 else "gla_on"), [], [gon])
        gbc = [ph.sb("gbc%d" % s, [128, D]) for s in range(2)]
        for s in range(2):
            self.ld(gbc[s][:], self.A("gbcd")[0, s], [self.R("gbcd")], [gbc[s]])
        r0p = ph.pool("r0", [128, 1024], F32, 2); r1p = ph.pool("r1", [128, 1024], F32, 2)
        zp = ph.pool("z", [128, 1024], BF16, 2)
        jkp = ph.pool("jk", [128, 1024], BF16, 1); ssp = ph.pool("ss", [128, 16], F32, 2)
        mxp = ph.pool("mx", [128, D], BF16, 2)
        mtp = ph.pool("mT", [128, KC, 128], BF16, 2)
        xp = ph.pool("x", [128, D], F32, 2); xop = ph.pool("xo", [128, D], F32, 2)
        tmp = ph.pool("tmp", [128, 512], F32, 2)
        pt = ph.psum("pt", [128, D], BF16, 1)
        pnt = ph.psum("pnt", [128, D], BF16, 1)
        py = ph.psum("py", [128, 512], F32, 3)
        npools = (ph.pool("njk", [128, D], BF16, 1), ph.pool("nxn", [128, D], BF16, 2), ph.pool("nst", [128, 2], F32, 4),
                  pnt, ph.pool("nh", [128, KC, 128], BF16, 2))
        if l == 1:
            lam_init = 0.8 - 0.6 * math.exp(-0.3 * l)
            dl = ph.sb("dl", [1, 256]); pr = ph.sb("pr", [1, 128]); sm = ph.sb("sm", [1, 4])
            nlam = ph.sb("nlam", [128, 1]); gsub = ph.sb("gsub", [128, 128])
            self.ld(dl[:], self.A("dlam"), [], [dl])
            self.ld(gsub[:], self.A("dsub_g"), [], [gsub])
            self.ts("dve", gsub[:], gsub[:], 1.0 - lam_init, None, ALU.mult, None, [gsub], [gsub])
            self.tt("dve", pr[:, 0:64], dl[:, 0:64], dl[:, 64:128], ALU.mult, [dl], [pr])
            self.tt("dve", pr[:, 64:128], dl[:, 128:192], dl[:, 192:256], ALU.mult, [dl], [pr])
            self.red(sm[:, 0:2], pr[:].rearrange("p (a d) -> p a d", a=2), [pr], [sm])
            self.act(sm[:, 0:2], sm[:, 0:2], AF.Exp, [sm], [sm])
            self.tt("dve", sm[:, 2:3], sm[:, 0:1], sm[:, 1:2], ALU.subtract, [sm], [sm])
            self.ts("dve", sm[:, 3:4], sm[:, 2:3], lam_init, None, ALU.add, None, [sm], [sm])
            pl = py.get()
            self.mm(pl[:, 0:1], self.ones[0:1, :], sm[0:1, 3:4], True, True, [self.ones, sm], [pl])
            self.ts("dve", nlam[:], pl[:, 0:1], -1.0, None, ALU.mult, None, [pl], [nlam])
            afp = ph.pool("af", [128, 16, 128], F32, 1); a8p = ph.pool("a8", [128, 8, 128], F32, 1)
            j2p = ph.pool("j2", [128, 8, 128], BF16, 1)
        for n in range(NT):
            if n < NTC and not need_ctx:
                continue
            s = 1 if n < NTC else 0
            ts_ = slice(n * 128, (n + 1) * 128)
            r0 = r0p.get(); r1 = r1p.get(); z = zp.get(); mx = mxp.get(); x = xp.get()
            self.ld(r0[:], self.A("rec")[0][ts_, :], [self.R("rec")], [r0])
            self.ld(r1[:], self.A("rec")[1][ts_, :], [self.R("rec")], [r1])
            self.ld(z[:], self.A("zs")[ts_, :], [self.R("zs")], [z])
            if l == 0:
                self.ld(mx[:, 0:1024], self.A("mix")[ts_, 0:1024], [self.R("mix")], [mx])
            else:
                af = afp.get(); a8 = a8p.get(); j2 = j2p.get(); ss2 = ssp.get()
                self.ld(af[:], self.A("attf")[ts_], [self.R("attf")], [af])
                afv = af[:].rearrange("p (h m) d -> p h m d", m=2)
                self.stt("dve", a8[:], afv[:, :, 1, :], nlam[:, 0:1], afv[:, :, 0, :], ALU.mult, ALU.add, [af, nlam], [a8])
                self.act(j2[:], a8[:], AF.Square, [a8], [j2])
                self.red(ss2[:, 0:8], j2[:], [j2], [ss2])
                self.rstd(ss2[:, 8:16], ss2[:, 0:8], 128, [ss2], [ss2])
                for h in range(8):
                    self.stt("dve", mx[:, h * 128:(h + 1) * 128], a8[:, h, :], ss2[:, 8 + h:9 + h], gsub[:], ALU.mult, ALU.mult,
                             [a8, ss2, gsub], [mx])
            self.ld(x[:], self.A(xin)[ts_, :], [self.R(xin)], [x])
            self.tt("pool", r0[:], r0[:], r1[:], ALU.add, [r0, r1], [r0])
            jk = jkp.get(); ss = ssp.get()
            self.act(jk[:], r0[:], AF.Square, [r0], [jk])
            self.red(ss[:, 0:nh], jk[:].rearrange("p (h d) -> p h d", h=nh), [jk], [ss])
            self.rstd(ss[:, 8:8 + nh], ss[:, 0:nh], dv, [ss], [ss])
            for h in range(nh):
                self.stt("dve", r1[:, h * dv:(h + 1) * dv], r0[:, h * dv:(h + 1) * dv], ss[:, 8 + h:9 + h], gon[:], ALU.mult, ALU.mult,
                         [r0, ss, gon], [r1])
            self.tt("dve", mx[:, 1024:2048], r1[:], z[:], ALU.mult, [r1, z], [mx])
            p = pt.get()
            for c in range(KC):
                self.tr(p[:, c * 128:(c + 1) * 128], mx[:, c * 128:(c + 1) * 128], self.identb[:], [mx, self.identb], [p])
            mT = mtp.get()
            self.cp("act", mT[:].rearrange("p c t -> p (c t)")[:, 0:1024], p[:, 0:1024], [p], [mT])
            self.cp("dve", mT[:].rearrange("p c t -> p (c t)")[:, 1024:2048], p[:, 1024:2048], [p], [mT])
            xo = xop.get()
            for cb in range(4):
                y = py.get()
                for k in range(KC):
                    self.mm(y[:], mT[:, k, :], wo[:, k, cb * 512:(cb + 1) * 512], k == 0, k == KC - 1, [mT, wo], [y])
                t = tmp.get()
                self.tt("dve", t[:], y[:], gbc[s][:, cb * 512:(cb + 1) * 512], ALU.mult, [y, gbc[s]], [t])
                self.tt("pool", xo[:, cb * 512:(cb + 1) * 512], t[:], x[:, cb * 512:(cb + 1) * 512], ALU.add, [t, x], [xo])
            self.st(self.A(xout)[ts_, :], xo[:], [xo], [self.R(xout)], q="act")
            self.norm_tile(ph, xo, s, 1, npools, "h2T", n)


def phase_ffn(self, l, xin, xout, need_ctx, final):
    with Phase(self, "F%d" % l) as ph:
        TC = self.TC
        NJ = FFN_H // 128
        wg = self.A("wg%d" % l).rearrange("(k p) n -> p k n", p=128)
        wu = self.A("wu%d" % l).rearrange("(k p) n -> p k n", p=128)
        wd = self.A("wd%d" % l).rearrange("(j p) n -> p j n", p=128)
        gbc = [ph.sb("gbc%d" % s, [128, D]) for s in range(2)]
        for s in range(2):
            self.ld(gbc[s][:], self.A("gbcd")[1, s], [self.R("gbcd")], [gbc[s]])
        hbp = ph.pool("hb", [128, KC, 512], BF16, 1)
        wgp = ph.pool("wg", [128, KC, 512], BF16, 2); wup = ph.pool("wu", [128, KC, 512], BF16, 2)
        actp = ph.pool("act", [128, NJ, 512], BF16, 1)
        sgp = ph.pool("sg", [128, 512], F32, 2)
        wdp = ph.pool("wd", [128, 11, 512], BF16, 2)
        xp = ph.pool("x", [128, 512], F32, 2); xop = ph.pool("xo", [128, 512], F32, 2); tp = ph.pool("t", [128, 512], F32, 2)
        pg = ph.psum("pg", [128, 512], F32, 2); pu = ph.psum("pu", [128, 512], F32, 2)
        pd = [ph.psum("pd%d" % i, [128, 512], F32, 1).get() for i in range(4)]
        for (t0, ntok, isctx) in self.blocks(512):
            if isctx and not need_ctx:
                continue
            s = 1 if isctx else 0
            hb = hbp.get()
            self.ld(hb[:, :, 0:ntok], self.A("h2T")[:, :, t0:t0 + ntok], [self.R("h2T")], [hb])
            at = actp.get()
            for j4 in range(NJ // 4):
                g_ = wgp.get(); u_ = wup.get()
                self.ld(g_[:], wg[:, :, j4 * 512:(j4 + 1) * 512], [self.R("wg%d" % l)], [g_])
                self.ld(u_[:], wu[:, :, j4 * 512:(j4 + 1) * 512], [self.R("wu%d" % l)], [u_])
                for jj in range(4):
                    j = j4 * 4 + jj
                    a = pg.get(); b = pu.get()
                    for k in range(KC):
                        self.mm(a[:, 0:ntok], g_[:, k, jj * 128:(jj + 1) * 128], hb[:, k, 0:ntok], k == 0, k == KC - 1, [g_, hb], [a])
                    for k in range(KC):
                        self.mm(b[:, 0:ntok], u_[:, k, jj * 128:(jj + 1) * 128], hb[:, k, 0:ntok], k == 0, k == KC - 1, [u_, hb], [b])
                    sg = sgp.get()
                    self.act(sg[:, 0:ntok], a[:, 0:ntok], AF.Silu, [a], [sg])
                    self.tt("dve", at[:, j, 0:ntok], sg[:, 0:ntok], b[:, 0:ntok], ALU.mult, [sg, b], [at])
            ntile = ntok // 128
            for cb in range(4):
                for pc in range(4):
                    w = wdp.get()
                    self.ld(w[:], wd[:, pc * 11:(pc + 1) * 11, cb * 512:(cb + 1) * 512], [self.R("wd%d" % l)], [w])
                    for ti in range(ntile):
                        for jj in range(11):
                            j = pc * 11 + jj
                            self.mm(pd[ti][:], at[:, j, ti * 128:(ti + 1) * 128], w[:, jj, :], j == 0, j == NJ - 1, [at, w], [pd[ti]])
                for ti in range(ntile):
                    tk = t0 + ti * 128
                    x = xp.get(); xo = xop.get(); t = tp.get()
                    self.ld(x[:], self.A(xin)[tk:tk + 128, cb * 512:(cb + 1) * 512], [self.R(xin)], [x])
                    self.tt("dve", t[:], pd[ti][:], gbc[s][:, cb * 512:(cb + 1) * 512], ALU.mult, [pd[ti], gbc[s]], [t])
                    self.tt("pool", xo[:], t[:], x[:], ALU.add, [t, x], [xo])
                    if final:
                        self.st(self.A("out")[tk - TC:tk - TC + 128, cb * 512:(cb + 1) * 512], xo[:], [xo], [self.R("out")], q="act")
                    else:
                        self.st(self.A(xout)[tk:tk + 128, cb * 512:(cb + 1) * 512], xo[:], [xo], [self.R(xout)], q="act")


Kern.norm_tile = norm_tile
Kern.phase_outproj = phase_outproj
Kern.phase_ffn = phase_ffn


def phase_inproj1(self):
    with Phase(self, "A1") as ph:
        TC = self.TC
        wsrc = self.A("cdin").rearrange("(k p) n -> p k n", p=128)
        RW = [self.R("cdin")]
        w_lr = ph.sb("w_lr", [128, KC, 32], BF16)
        self.ld(w_lr[:], wsrc[:, :, 6144:6176], RW, [w_lr])
        cst = {}
        for nm in ("dq_g", "dk_g"):
            cst[nm] = ph.sb(nm, [128, 128])
            self.ld(cst[nm][:], self.A(nm), [], [cst[nm]])
        hbp = ph.pool("hb", [128, KC, 512], BF16, 1)
        wfp = ph.pool("wf", [128, KC, 128], BF16, 2)
        wtp = ph.pool("wt", [128, KC, 512], BF16, 2)
        psA = ph.psum("psA", [128, 512], F32, 4)
        psT = ph.psum("psT", [128, 1024], BF16, 2)
        fo = ph.pool("fo", [128, 512], F32, 3)
        pre = ph.pool("pre", [128, 8, 64], F32, 2)
        hpools = (ph.pool("hj", [128, 8, 64], BF16, 1), ph.pool("hss", [128, 32], F32, 4),
                  ph.pool("ho", [128, 8, 64], BF16, 2), ph.pool("ht", [128, 8, 128], F32, 2))
        vvp = ph.pool("vv", [128, 4, 130], BF16, 2)
        trp = ph.pool("trp", [64, 8, 128], BF16, 2)
        zp = ph.pool("zp", [128, 512], BF16, 2)
        rc = ph.pool("rc", [128, 64], F32, 4); rs = ph.pool("rs", [128, 64], F32, 4)
        for (t0, ntok, isctx) in self.blocks(512):
            hb = hbp.get()
            self.ld(hb[:, :, 0:ntok], self.A("hT")[:, :, t0:t0 + ntok], [self.R("hT")], [hb])
            ntile = ntok // 128
            cs = [None] * ntile
            if not isctx:
                for ti in range(ntile):
                    tk = t0 + ti * 128
                    C = rc.get(); S = rs.get()
                    self.ld(C[:], self.A("ropeC")[tk - TC:tk - TC + 128, :], [], [C])
                    self.ld(S[:], self.A("ropeS")[tk - TC:tk - TC + 128, :], [], [S])
                    cs[ti] = (C, S)
            for ch in range(8):
                w = wfp.get()
                self.ld(w[:], wsrc[:, :, 3072 + ch * 128:3072 + (ch + 1) * 128], RW, [w])
                p = psA.get()
                for k in range(KC):
                    self.mm(p[:, 0:ntok], w[:, k, :], hb[:, k, 0:ntok], k == 0, k == KC - 1, [w, hb], [p])
                f = fo.get()
                if ch < 4:
                    self.act(f[:, 0:ntok], p[:, 0:ntok], AF.Copy, [p], [f], scale=128 ** -0.5)
                    self.st(self.A("glq")[ch][:, t0:t0 + ntok], f[:, 0:ntok], [f], [self.R("glq")])
                else:
                    self.cp("dve", f[:, 0:ntok], p[:, 0:ntok], [p], [f])
                    self.st(self.A("glk")[ch - 4][:, t0:t0 + ntok], f[:, 0:ntok], [f], [self.R("glk")])
            p = psA.get()
            for k in range(KC):
                self.mm(p[0:32, 0:ntok], w_lr[:, k, :], hb[:, k, 0:ntok], k == 0, k == KC - 1, [w_lr, hb], [p])
            f = fo.get()
            self.cp("dve", f[0:32, 0:ntok], p[0:32, 0:ntok], [p], [f])
            self.st(self.A("glr")[:, t0:t0 + ntok], f[0:32, 0:ntok], [f], [self.R("glr")])
            jobs = [("dq", 0, 0), ("dq", 512, 1), ("dk", 1024, 0), ("dk", 1536, 1), ("dv", 2048, 0), ("dv", 2560, 1),
                    ("gk", 3584, 0), ("gv", 4096, 0), ("gv", 4608, 1), ("gg", 5120, 0), ("gg", 5632, 1)]
            for (kind, c0, idx) in jobs:
                wt = wtp.get()
                self.ld(wt[:], wsrc[:, :, c0:c0 + 512], RW, [wt])
                for ti in range(ntile):
                    tk = t0 + ti * 128
                    p = psA.get()
                    for k in range(KC):
                        self.mm(p[:], hb[:, k, ti * 128:(ti + 1) * 128], wt[:, k, :], k == 0, k == KC - 1, [hb, wt], [p])
                    if kind in ("dq", "dk"):
                        pr = pre.get()
                        self.cp("act", pr[:].rearrange("p g d -> p (g d)"), p[:], [p], [pr])
                        gname = "dq_g" if kind == "dq" else "dk_g"
                        g_ = cst[gname]
                        cs_t = cs[ti]
                        o = self.head_norm_rope(ph, pr, 8, 64, g_, not isctx, cs_t, hpools, kind,
                                                gsel=lambda h, g_=g_: g_[:, (h % 2) * 64:(h % 2 + 1) * 64])
                        pt = psT.get()
                        for g in range(8):
                            self.tr(pt[0:64, g * 128:(g + 1) * 128], o[:, g, :], self.identb[:], [o, self.identb], [pt])
                        a2 = trp.get()
                        self.cp("dve", a2[:].rearrange("p h t -> p (h t)"), pt[0:64, 0:1024], [pt], [a2])
                        dn = "qT" if kind == "dq" else "kT"
                        self.st(self.A(dn)[idx * 8:(idx + 1) * 8].rearrange("g p t -> p g t")[0:64, :, tk:tk + 128], a2[:], [a2], [self.R(dn)])
                    elif kind == "dv":
                        v = vvp.get()
                        self.cp("act", v[:, :, 0:128], p[:].rearrange("p (h d) -> p h d", h=4), [p], [v])
                        self.memset("pool", v[:, :, 128:130], 1.0, [v])
                        self.st(self.A("vv")[tk:tk + 128, idx * 4:(idx + 1) * 4, :], v[:], [v], [self.R("vv")])
                    elif kind == "gg":
                        z = zp.get()
                        self.act(z[:], p[:], AF.Silu, [p], [z])
                        self.st(self.A("zs")[tk:tk + 128, idx * 512:(idx + 1) * 512], z[:], [z], [self.R("zs")])
                    else:
                        f = fo.get()
                        self.cp("act" if ti % 2 else "dve", f[:], p[:], [p], [f])
                        if kind == "gk":
                            self.st(self.A("glkt")[tk:tk + 128, :], f[:], [f], [self.R("glkt")])
                        else:
                            self.st(self.A("glv")[tk:tk + 128, idx * 512:(idx + 1) * 512], f[:], [f], [self.R("glv")])


def phase_gla_scan(self):
    with Phase(self, "L2") as ph:
        NT, NTC = self.NT, self.NTC
        M = self.masks
        pp = ph.psum("pp", [128, 512], F32, 7)
        S = [[ph.sb("S%d%d" % (d, h), [128, 256]) for h in range(4)] for d in range(2)]
        for d in range(2):
            for h in range(4):
                self.memset("pool", S[d][h][:], 0.0, [S[d][h]])
        gw = ph.sb("gw", [33, 2, 512])
        self.ld(gw[:], self.A("gw2").rearrange("d k n -> k d n"), [], [gw])
        lrp = ph.pool("lr", [33, 128], F32, 2)
        qTp = ph.pool("qT", [128, 4, 128], F32, 2); kTp = ph.pool("kT", [128, 4, 128], F32, 2)
        ktp = ph.pool("kt", [128, 512], F32, 2); vp = ph.pool("v", [128, 1024], F32, 2)
        glp = ph.pool("gl", [128, 512], F32, 2)
        e1p = ph.pool("e1", [128, 128], F32, 2); e2p = ph.pool("e2", [128, 128], F32, 2)
        qtp = ph.pool("qt", [128, 128], F32, 2); ktlp = ph.pool("ktl", [128, 128], F32, 2)
        amp = ph.pool("am", [128, 128], F32, 2); dfp = ph.pool("df", [128, 128], F32, 2); kep = ph.pool("ke", [128, 128], F32, 2)
        ecp = ph.pool("ec", [128, 1], F32, 4)
        outp = ph.pool("out", [128, 1024], F32, 2)
        order = {0: list(range(NT)), 1: list(range(NTC - 1, -1, -1)) + list(range(NT - 1, NTC - 1, -1))}
        for step in range(NT):
            for d in range(2):
                n = order[d][step]
                with_out = n >= NTC
                ts_ = slice(n * 128, (n + 1) * 128)
                lr = lrp.get(); qT = qTp.get(); kT = kTp.get(); kt = ktp.get(); v = vp.get()
                self.ld(lr[0:32, :], self.A("glr")[:, ts_], [self.R("glr")], [lr])
                self.memset("pool", lr[32:33, :], 1.0, [lr])
                if with_out:
                    self.ld(qT[:], self.A("glq").rearrange("h p t -> p h t")[:, :, ts_], [self.R("glq")], [qT])
                    self.ld(kT[:], self.A("glk").rearrange("h p t -> p h t")[:, :, ts_], [self.R("glk")], [kT])
                self.ld(kt[:], self.A("glkt")[ts_, :], [self.R("glkt")], [kt])
                self.ld(v[:], self.A("glv")[ts_, :], [self.R("glv")], [v])
                pl = pp.get()
                self.mm(pl[:], lr[:], gw[:, d, :], True, True, [lr, gw], [pl])
                gl = glp.get()
                self.act(gl[:], pl[:], AF.Exp, [pl], [gl], scale=-1.0)
                self.act(gl[:], gl[:], AF.Ln, [gl], [gl], bias=1.0)
                self.ts("dve", gl[:], gl[:], -1.0 / 16.0, None, ALU.mult, None, [gl], [gl])
                tri = M[:, 1, :] if d == 0 else M[:, 0, :]
                out = outp.get() if with_out else None
                for h in range(4):
                    glh = gl[:, h * 128:(h + 1) * 128]
                    pc = pp.get()
                    self.mm(pc[:, 0:128], tri, glh, True, True, [M, gl], [pc])
                    self.mm(pc[:, 128:256], self.ones[:], glh, True, True, [self.ones, gl], [pc])
                    self.mm(pc[:, 384:385], glh, self.ones[:, 0:1], True, True, [gl, self.ones], [pc])
                    if with_out:
                        self.mm(pc[:, 256:384], glh, tri, True, True, [gl, M], [pc])
                    df = dfp.get(); ke = kep.get()
                    cumc = e1p.get()
                    self.cp("act", cumc[:], pc[:, 0:128], [pc], [cumc])
                    self.tt("dve", df[:], pc[:, 128:256], cumc[:], ALU.subtract, [pc, cumc], [df])
                    self.act(df[:], df[:], AF.Exp, [df], [df])
                    self.tt("pool", ke[:], kt[:, h * 128:(h + 1) * 128], df[:], ALU.mult, [kt, df], [ke])
                    ec = ecp.get()
                    self.act(ec[:], pc[:, 384:385], AF.Exp, [pc], [ec])
                    St = S[d][h]
                    vh = v[:, h * 256:(h + 1) * 256]
                    if with_out:
                        e1 = e1p.get(); e2 = e2p.get(); qt = qtp.get(); ktl = ktlp.get()
                        self.act(e1[:], pc[:, 256:384], AF.Exp, [pc], [e1])
                        self.act(e2[:], pc[:, 256:384], AF.Exp, [pc], [e2], scale=-1.0)
                        self.tt("dve", qt[:], qT[:, h, :], e1[:], ALU.mult, [qT, e1], [qt])
                        self.tt("pool", ktl[:], kT[:, h, :], e2[:], ALU.mult, [kT, e2], [ktl])
                        pa = pp.get()
                        self.mm(pa[:, 0:128], ktl[:], qt[:], True, True, [ktl, qt], [pa])
                        am = amp.get()
                        ib = M[:, 1, :] if d == 0 else M[:, 0, :]
                        self.tt("dve", am[:], pa[:, 0:128], ib, ALU.mult, [pa, M], [am])
                        po = pp.get()
                        self.mm(po[:, 0:256], qt[:], St[:], True, False, [qt, St], [po])
                        self.mm(po[:, 0:256], am[:], vh, False, True, [am, v], [po])
                        self.cp("act", out[:, h * 256:(h + 1) * 256], po[:, 0:256], [po], [out])
                    pk = pp.get()
                    self.mm(pk[:, 0:256], ke[:], vh, True, True, [ke, v], [pk])
                    self.stt("dve", St[:], St[:], ec[:, 0:1], pk[:, 0:256], ALU.mult, ALU.add, [St, ec, pk], [St])
                if with_out:
                    self.st(self.A("rec")[d][ts_, :], out[:], [out], [self.R("rec")])


Kern.phase_inproj1 = phase_inproj1
Kern.phase_gla_scan = phase_gla_scan


_CACHE = {}


def kernel(**inputs):
    TL, TC, B = 4096, 256, 4
    inp = {k: np.asarray(v) for k, v in inputs.items()}
    if "nc" not in _CACHE:
        kk = Kern(TL, TC)
        _CACHE["nc"] = kk.build()
        _CACHE["names"] = list(kk.in_names)
    nc = _CACHE["nc"]
    names = _CACHE["names"]
    maps = []
    for b in range(B):
        m = prep_core_inputs(inp, b, TL, TC)
        maps.append({k: m[k] for k in names})
    in_maps = [maps[c % B] for c in range(8)]
    res = run_bass_kernel_spmd(nc, in_maps, core_ids=list(range(8)))
    out = np.stack([np.asarray(res.results[b]["out"], dtype=np.float32) for b in range(B)], axis=0)
    return out
```

```python
import math
import numpy as np
from contextlib import ExitStack
import concourse.bass as bass
import concourse.mybir as mybir
from concourse.bass_utils import run_bass_kernel_spmd

F32 = mybir.dt.float32
BF16 = mybir.dt.bfloat16
ALU = mybir.AluOpType
AF = mybir.ActivationFunctionType
AX = mybir.AxisListType

D = 2048
KC = 16
NMOD = 6
EPS = 1e-6
FFN_H = 5632
AB_IN = 5216
CD_IN = 6176
NDMASEM = 12


class Buf:
    __slots__ = ("name", "w", "r", "t")

    def __init__(self, name, t=None):
        self.name = name
        self.w = None
        self.r = []
        self.t = t

    def __getitem__(self, idx):
        return self.t[idx]


class Prog:
    ENG = ("pe", "dve", "act", "pool", "sp")

    def __init__(self, nc, es):
        self.nc = nc
        self.sem = {}
        self.cnt = {}
        for e in ("pe", "dve", "act", "pool"):
            self.sem[e] = es.enter_context(nc.semaphore("s_" + e))
            self.cnt[e] = 0
        self.dsem = {}
        for q in ("sp", "pool", "act"):
            for i in range(NDMASEM):
                k = "d_%s_%d" % (q, i)
                self.sem[k] = es.enter_context(nc.semaphore(k))
                self.cnt[k] = 0
            self.dsem[q] = 0
        self.known = {e: {} for e in self.ENG}
        self.lists = {e: [] for e in self.ENG}
        self.ninst = 0

    def _wait(self, E, key, val):
        if val <= self.known[E].get(key, 0):
            return
        self.known[E][key] = val
        sem = self.sem[key]
        self.lists[E].append(lambda eng, sem=sem, val=val: eng.wait_ge(sem, val))

    def _deps(self, E, reads, writes, is_dma):
        for b in reads:
            if b.w is not None:
                k, v, src = b.w
                if src == E and E == "pe" and not is_dma:
                    continue
                self._wait(E, k, v)
        for b in writes:
            if b.w is not None:
                k, v, src = b.w
                if not (src == E and E == "pe" and not is_dma):
                    self._wait(E, k, v)
            for (k, v, src) in b.r:
                if src == E and not is_dma and not k.startswith("d_"):
                    continue
                self._wait(E, k, v)

    def op(self, E, fn, reads=(), writes=()):
        self._deps(E, reads, writes, False)
        self.cnt[E] += 1
        v = self.cnt[E]
        sem = self.sem[E]
        self.lists[E].append(lambda eng, fn=fn, sem=sem: fn(eng).then_inc(sem, 1))
        ev = (E, v, E)
        for b in reads:
            b.r.append(ev)
        for b in writes:
            b.w = ev
            b.r = []
        self.ninst += 1

    def dma(self, Q, out_ap, in_ap, reads=(), writes=()):
        self._deps(Q, reads, writes, True)
        i = self.dsem[Q]
        self.dsem[Q] = (i + 1) % NDMASEM
        key = "d_%s_%d" % (Q, i)
        self._wait(Q, key, self.cnt[key])
        self.cnt[key] += 16
        v = self.cnt[key]
        sem = self.sem[key]
        self.lists[Q].append(lambda eng, o=out_ap, a=in_ap, sem=sem: eng.dma_start(out=o, in_=a).then_inc(sem, 16))
        ev = (key, v, Q)
        for b in reads:
            b.r.append(ev)
        for b in writes:
            b.w = ev
            b.r = []
        self.ninst += 1

    def drain_all(self):
        for E in self.ENG:
            for k in list(self.sem.keys()):
                if self.cnt[k] > 0:
                    self._wait(E, k, self.cnt[k])

    def flush(self):
        nc = self.nc
        lists = self.lists
        with nc.Block() as block:
            @block.tensor
            def _(e):
                for f in lists["pe"]:
                    f(e)

            @block.vector
            def _(e):
                for f in lists["dve"]:
                    f(e)

            @block.scalar
            def _(e):
                for f in lists["act"]:
                    f(e)

            @block.gpsimd
            def _(e):
                for f in lists["pool"]:
                    f(e)

            @block.sync
            def _(e):
                for f in lists["sp"]:
                    f(e)
        self.lists = {e: [] for e in self.ENG}


class Phase:
    def __init__(self, K, name):
        self.K = K
        self.name = name
        self.es = ExitStack()
        self.n = 0

    def __enter__(self):
        self.es.__enter__()
        return self

    def __exit__(self, *a):
        if a[0] is None:
            self.K.P.drain_all()
            self.K.P.flush()
            self.K.marks.append((self.name, dict(self.K.P.cnt)))
        return self.es.__exit__(*a)

    def sb(self, name, shape, dt=F32):
        self.n += 1
        nm = "%s_%s_%d" % (self.name, name, self.n)
        return Buf(nm, self.es.enter_context(self.K.nc.sbuf_tensor(nm, shape, dt)))

    def pool(self, name, shape, dt=F32, bufs=2):
        return Rot([self.sb(name, shape, dt) for _ in range(bufs)])

    def psum(self, name, shape, dt=F32, bufs=1):
        out = []
        for _ in range(bufs):
            self.n += 1
            nm = "%s_%s_%d" % (self.name, name, self.n)
            out.append(Buf(nm, self.es.enter_context(self.K.nc.psum_tensor(nm, shape, dt))))
        return Rot(out)


class Rot:
    def __init__(self, bufs):
        self.bufs = bufs
        self.i = 0

    def get(self):
        b = self.bufs[self.i]
        self.i = (self.i + 1) % len(self.bufs)
        return b


class Kern:
    def __init__(self, TL, TC, debug=(), layers=(0, 1)):
        self.TL, self.TC = TL, TC
        self.T = TL + TC
        self.NT = self.T // 128
        self.NTC = TC // 128
        self.debug = set(debug)
        self.layers = layers
        self.nc = bass.Bass("TRN2", target_bir_lowering=False)
        self.es = ExitStack()
        self.P = Prog(self.nc, self.es)
        self.aps = {}
        self.dbuf = {}
        self.in_names = []
        self.out_names = []
        self.marks = []
        self.gdn_hg = 4
        self.gdn_pack = False
        self.gdn_nb = 8

    def din(self, name, shape, dt=F32):
        self.aps[name] = self.nc.dram_tensor(name, list(shape), dt, kind="ExternalInput").ap()
        self.dbuf[name] = Buf(name)
        self.in_names.append(name)
        return self.aps[name]

    def dscr(self, name, shape, dt=F32, out=False):
        kind = "ExternalOutput" if (out or name in self.debug) else "Internal"
        self.aps[name] = self.nc.dram_tensor(name, list(shape), dt, kind=kind).ap()
        self.dbuf[name] = Buf(name)
        if kind == "ExternalOutput":
            self.out_names.append(name)
        return self.aps[name]

    def mm(self, out, lhsT, rhs, start, stop, R, W):
        self.P.op("pe", lambda e: e.matmul(out, lhsT=lhsT, rhs=rhs, start=start, stop=stop), R, W)

    def tr(self, out, in_, ident, R, W):
        self.P.op("pe", lambda e: e.transpose(out, in_, ident), R, W)

    def act(self, out, in_, func, R, W, scale=1.0, bias=0.0, accum=None):
        if accum is None:
            self.P.op("act", lambda e: e.activation(out=out, in_=in_, func=func, scale=scale, bias=bias), R, W)
        else:
            self.P.op("act", lambda e: e.activation(out=out, in_=in_, func=func, scale=scale, bias=bias,
                                                    accum_out=accum), R, W)

    def ts(self, E, out, in0, s1, s2, op0, op1, R, W):
        E = "dve"
        if s2 is None:
            self.P.op(E, lambda e: e.tensor_scalar(out=out, in0=in0, scalar1=s1, scalar2=None, op0=op0), R, W)
        else:
            self.P.op(E, lambda e: e.tensor_scalar(out=out, in0=in0, scalar1=s1, scalar2=s2, op0=op0, op1=op1), R, W)

    def tt(self, E, out, in0, in1, op, R, W):
        self.P.op(E, lambda e: e.tensor_tensor(out=out, in0=in0, in1=in1, op=op), R, W)

    def stt(self, E, out, in0, scalar, in1, op0, op1, R, W):
        E = "dve"
        self.P.op(E, lambda e: e.scalar_tensor_tensor(out=out, in0=in0, scalar=scalar, in1=in1, op0=op0, op1=op1), R, W)

    def cp(self, E, out, in_, R, W):
        if E == "act":
            self.P.op("act", lambda e: e.copy(out=out, in_=in_), R, W)
        else:
            self.P.op(E, lambda e: e.tensor_copy(out=out, in_=in_), R, W)

    def red(self, out, in_, R, W, op=ALU.add):
        self.P.op("dve", lambda e: e.tensor_reduce(out=out, in_=in_, axis=AX.X, op=op), R, W)

    def recip(self, out, in_, R, W):
        self.P.op("dve", lambda e: e.reciprocal(out=out, in_=in_), R, W)

    def memset(self, E, out, val, W):
        self.P.op(E, lambda e: e.memset(out, val), (), W)

    def ld(self, out, in_, R, W, q="sp"):
        self.P.dma(q, out, in_, R, W)

    def st(self, out, in_, R, W, q="pool"):
        self.P.dma(q, out, in_, R, W)

    def rstd(self, out, ss, n, R, W):
        self.act(out, ss, AF.Ln, R, W, scale=1.0 / n, bias=self.epsc[:, 0:1])
        self.act(out, out, AF.Exp, W, W, scale=-0.5)

    def declare(self):
        TL, TC, T = self.TL, self.TC, self.T
        d = self.din
        d("x", [TL, D]); d("ctx", [TC, D])
        d("cT", [128, 2, KC])
        d("ada_w", [2, D, NMOD * D]); d("ada_bT", [2, 128, NMOD * KC])
        d("gmixT", [2, 128, KC]); d("gffnT", [2, 128, KC])
        d("ffn_w_gate", [2, D, FFN_H]); d("ffn_w_up", [2, D, FFN_H]); d("ffn_w_down", [2, FFN_H, D])
        d("ab_w_in", [D, AB_IN]); d("mla_w_uq", [512, 1536]); d("mla_w_ukv", [512, 2048]); d("ab_w_out", [D, D])
        d("cd_w_in", [D, CD_IN]); d("cd_w_out", [D, D])
        d("qa_g", [128, 512]); d("kva_g", [128, 512]); d("q_g", [128, 192]); d("k_g", [128, 192])
        d("gdn_on", [128, 128]); d("gdn_alog", [128, 16]); d("gdn_dtb", [128, 16]); d("convT", [128, 24, 5])
        d("dq_g", [128, 128]); d("dk_g", [128, 128]); d("dsub_g", [128, 128]); d("gla_on", [128, 256])
        d("dlam", [1, 256]); d("gw2", [2, 33, 512])
        d("ropeC", [TL, 64]); d("ropeS", [TL, 64])
        d("ident", [128, 128]); d("masks", [6, 128, 128])
        s = self.dscr
        s("xs", [T, D]); s("xb", [T, D])
        s("out", [TL, D], out=True)
        for l in (0, 1):
            s("adawb%d" % l, [D, NMOD * D], BF16)
            s("wg%d" % l, [D, FFN_H], BF16); s("wu%d" % l, [D, FFN_H], BF16); s("wd%d" % l, [FFN_H, D], BF16)
        s("abin", [D, AB_IN], BF16); s("uq", [512, 1536], BF16); s("ukv", [512, 2048], BF16); s("about", [D, D], BF16)
        s("cdin", [D, CD_IN], BF16); s("cdout", [D, D], BF16)
        s("hT", [128, KC, T], BF16)
        s("qT", [16, 128, T], BF16)
        s("qTr", [8, 64, T], BF16)
        s("kT", [16, 128, T], BF16)
        s("kTr", [8, 64, T], BF16)
        s("vv", [T, 8, 130], BF16)
        s("mix", [T, D], BF16)
        s("attf", [T, 16, 128], F32)
        s("gqkv", [24, 128, T], F32)
        s("gcv", [24, 128, T], F32)
        s("zs", [T, 1024], BF16)
        s("gb", [T, 32], F32)
        s("rec", [2, T, 1024], F32)
        s("glq", [4, 128, T], F32); s("glk", [4, 128, T], F32)
        s("glkt", [T, 512], F32); s("glv", [T, 1024], F32)
        s("gllog", [2, T, 512], F32)
        s("glr", [32, T], F32)
        s("h2T", [128, KC, T], BF16)
        s("gbcd", [2, 2, 128, D])

    def R(self, name):
        return self.dbuf[name]

    def A(self, name):
        return self.aps[name]

    def consts(self):
        nc, es = self.nc, self.es

        def sb(name, shape, dt=F32):
            return Buf(name, es.enter_context(nc.sbuf_tensor(name, shape, dt)))
        self.ident = sb("identf", [128, 128])
        self.identb = sb("identb", [128, 128], BF16)
        self.ones = sb("ones", [128, 128])
        self.onesb = sb("onesb", [128, 128], BF16)
        self.epsc = sb("epsc", [128, 1])
        self.masks = sb("masksb", [128, 6, 128])
        self.mod = sb("mod", [128, 2, NMOD * KC])
        self.gm = sb("gm", [128, 2, 2, KC])
        self.cT = sb("cTs", [128, 2, KC])
        self.scb = sb("scb", [128, 2, KC], BF16)

    def phase_setup(self):
        with Phase(self, "S") as ph:
            self.ld(self.ident[:], self.A("ident"), [], [self.ident])
            self.ld(self.masks[:], self.A("masks").rearrange("m p n -> p m n"), [], [self.masks])
            self.ld(self.cT[:], self.A("cT"), [], [self.cT])
            self.cp("dve", self.identb[:], self.ident[:], [self.ident], [self.identb])
            self.memset("dve", self.ones[:], 1.0, [self.ones])
            self.memset("dve", self.onesb[:], 1.0, [self.onesb])
            self.memset("dve", self.epsc[:], EPS, [self.epsc])
            self.act(self.scb[:], self.cT[:], AF.Silu, [self.cT], [self.scb])
            self.ld(self.A("xs")[0:self.TC, :], self.A("ctx"), [], [self.R("xs")])
            self.ld(self.A("xs")[self.TC:self.T, :], self.A("x"), [], [self.R("xs")])
            stg = ph.pool("stg", [128, 2048], F32, 3)
            stb = ph.pool("stb", [128, 2048], BF16, 3)
            jobs = []
            for l in self.layers:
                jobs.append((self.A("ada_w")[l], "adawb%d" % l, D, NMOD * D))
                jobs.append((self.A("ffn_w_gate")[l], "wg%d" % l, D, FFN_H))
                jobs.append((self.A("ffn_w_up")[l], "wu%d" % l, D, FFN_H))
                jobs.append((self.A("ffn_w_down")[l], "wd%d" % l, FFN_H, D))
            if 0 in self.layers:
                jobs += [(self.A("ab_w_in"), "abin", D, AB_IN), (self.A("mla_w_uq"), "uq", 512, 1536),
                         (self.A("mla_w_ukv"), "ukv", 512, 2048), (self.A("ab_w_out"), "about", D, D)]
            if 1 in self.layers:
                jobs += [(self.A("cd_w_in"), "cdin", D, CD_IN), (self.A("cd_w_out"), "cdout", D, D)]
            i = 0
            engs = ("dve", "act", "pool")
            for (src, dn, K_, N_) in jobs:
                dst = self.A(dn)
                for k in range(K_ // 128):
                    for c0 in range(0, N_, 2048):
                        w = min(2048, N_ - c0)
                        a = stg.get(); b = stb.get()
                        self.ld(a[:, 0:w], src[k * 128:(k + 1) * 128, c0:c0 + w], [], [a])
                        self.cp(engs[i % 3], b[:, 0:w], a[:, 0:w], [a], [b])
                        self.st(dst[k * 128:(k + 1) * 128, c0:c0 + w], b[:, 0:w], [b], [self.R(dn)], q="act" if i % 2 else "pool")
                        i += 1

    def phase_mod(self, l):
        with Phase(self, "M%d" % l) as ph:
            wp = ph.pool("w", [128, KC, 512], BF16, 3)
            ps = ph.psum("ps", [128, 512], F32, 1).get()
            adab = ph.sb("adab", [128, NMOD * KC])
            gmix = ph.sb("gmix", [128, KC]); gffn = ph.sb("gffn", [128, KC])
            self.ld(adab[:], self.A("ada_bT")[l], [], [adab])
            self.ld(gmix[:], self.A("gmixT")[l], [], [gmix])
            self.ld(gffn[:], self.A("gffnT")[l], [], [gffn])
            wsrc = self.A("adawb%d" % l).rearrange("(k p) n -> p k n", p=128)
            for cb in range(NMOD * D // 512):
                w = wp.get()
                self.ld(w[:], wsrc[:, :, cb * 512:(cb + 1) * 512], [self.R("adawb%d" % l)], [w])
                for jj in range(4):
                    j = cb * 4 + jj
                    for k in range(KC):
                        self.mm(ps[:, 2 * j:2 * j + 2], w[:, k, jj * 128:(jj + 1) * 128], self.scb[:, :, k],
                                k == 0, k == KC - 1, [w, self.scb], [ps])
            psv = ps[:, 0:2 * NMOD * KC].rearrange("p (j s) -> p s j", s=2)
            for s in range(2):
                self.tt("dve", self.mod[:, s, :], psv[:, s, :], adab[:], ALU.add, [ps, adab], [self.mod])
            for s in range(2):
                self.stt("dve", self.gm[:, 0, s, :], self.mod[:, s, 1 * KC:2 * KC], 1.0, gmix[:], ALU.add, ALU.mult,
                         [self.mod, gmix], [self.gm])
                self.stt("dve", self.gm[:, 1, s, :], self.mod[:, s, 4 * KC:5 * KC], 1.0, gffn[:], ALU.add, ALU.mult,
                         [self.mod, gffn], [self.gm])
            dg = ph.pool("dg", [128, 512], F32, 2)
            gbo = ph.pool("gbo", [128, 512], F32, 2)
            pb = ph.psum("pb", [128, 512], F32, 2)
            for which, mi in ((0, 2), (1, 5)):
                for s in range(2):
                    for c4 in range(KC // 4):
                        dgt = dg.get()
                        for q in range(4):
                            c = c4 * 4 + q
                            self.ts("dve", dgt[:, q * 128:(q + 1) * 128], self.ident[:],
                                    self.mod[:, s, mi * KC + c:mi * KC + c + 1], None, ALU.mult, None,
                                    [self.ident, self.mod], [dgt])
                        p = pb.get()
                        self.mm(p[:], self.ones[:], dgt[:], True, True, [self.ones, dgt], [p])
                        go = gbo.get()
                        self.cp("act", go[:], p[:], [p], [go])
                        self.st(self.A("gbcd")[which, s][:, c4 * 512:(c4 + 1) * 512], go[:], [go], [self.R("gbcd")])

    def phase_norm(self, which, dstname, l):
        with Phase(self, "N%d%d" % (which, l)) as ph:
            xp = ph.pool("x", [128, D], F32, 3)
            jk = ph.pool("jk", [128, D], BF16, 2)
            xn = ph.pool("xn", [128, D], BF16, 2)
            st_ = ph.pool("st", [128, 2], F32, 4)
            pt = ph.psum("pt", [128, D], BF16, 2)
            hp = ph.pool("h", [128, KC, 128], BF16, 3)
            sh_i = 0 if which == 0 else 3
            for n in range(self.NT):
                s = 1 if n < self.NTC else 0
                x = xp.get(); j = jk.get(); xx = xn.get(); stt_ = st_.get(); p = pt.get(); h = hp.get()
                self.ld(x[:], self.A("xs")[n * 128:(n + 1) * 128, :], [self.R("xs")], [x])
                self.act(j[:], x[:], AF.Square, [x], [j, stt_], accum=stt_[:, 0:1])
                self.rstd(stt_[:, 1:2], stt_[:, 0:1], D, [stt_], [stt_])
                self.ts("dve", xx[:], x[:], stt_[:, 1:2], None, ALU.mult, None, [x, stt_], [xx])
                for c in range(KC):
                    self.tr(p[:, c * 128:(c + 1) * 128], xx[:, c * 128:(c + 1) * 128], self.identb[:], [xx, self.identb], [p])
                for c in range(KC):
                    gcol = self.gm[:, which, s, c:c + 1]
                    scol = self.mod[:, s, sh_i * KC + c:sh_i * KC + c + 1]
                    if c % 2 == 0:
                        self.act(h[:, c, :], p[:, c * 128:(c + 1) * 128], AF.Identity, [p, self.gm, self.mod], [h],
                                 scale=gcol, bias=scol)
                    else:
                        self.ts("dve", h[:, c, :], p[:, c * 128:(c + 1) * 128], gcol, scol, ALU.mult, ALU.add,
                                [p, self.gm, self.mod], [h])
                self.st(self.A(dstname)[:, :, n * 128:(n + 1) * 128], h[:], [h], [self.R(dstname)])

    def blocks(self, bs=512):
        out = []
        t = 0
        while t < self.TC:
            n = min(bs, self.TC - t); out.append((t, n, True)); t += n
        while t < self.T:
            n = min(bs, self.T - t); out.append((t, n, False)); t += n
        return out

    @staticmethod
    def bcast(ap, axis, n):
        a = ap.unsqueeze(axis)
        shp = list(a.shape)
        shp[axis] = n
        return a.broadcast_to(shp)

    def head_norm_rope(self, ph, pre, nh, hd, gbc, rope, cs, pools, tag, gsel=None):
        jk, ssp, outp, tmpp = pools
        if gsel is None:
            gsel = lambda h: gbc[:, 0:hd]
        j = jk.get(); ss = ssp.get(); o = outp.get()
        self.act(j[:, 0:nh, 0:hd], pre[:, 0:nh, 0:hd], AF.Square, [pre], [j])
        self.red(ss[:, 0:nh], j[:, 0:nh, 0:hd], [j], [ss])
        self.rstd(ss[:, 16:16 + nh], ss[:, 0:nh], hd, [ss], [ss])
        if rope:
            f = tmpp.get()
            for h in range(nh):
                self.stt("dve", f[:, h, 0:hd], pre[:, h, 0:hd], ss[:, 16 + h:17 + h], gsel(h),
                         ALU.mult, ALU.mult, [pre, ss, gbc], [f])
            C, S = cs
            r0 = hd - 64
            if r0 > 0:
                self.cp("act", o[:, 0:nh, 0:r0], f[:, 0:nh, 0:r0], [f], [o])
            t1 = tmpp.get()
            xr = f[:, 0:nh, r0:hd]
            self.tt("dve", t1[:, 0:nh, 0:64], xr, self.bcast(C[:, 0:64], 1, nh), ALU.mult, [f, C], [t1])
            xv = f[:, 0:nh, r0:hd].rearrange("p h (a b c) -> p h a b c", a=2, b=2)
            tv = t1[:, 0:nh, 64:128].rearrange("p h (a b c) -> p h a b c", a=2, b=2)
            Sv = S[:, 0:64].rearrange("p (a b c) -> p a b c", a=2, b=2)
            for a in range(2):
                for b in range(2):
                    self.tt("pool", tv[:, :, a, b, :], xv[:, :, a, 1 - b, :], self.bcast(Sv[:, a, b, :], 1, nh), ALU.mult,
                            [f, S], [t1])
            self.tt("dve", o[:, 0:nh, r0:hd], t1[:, 0:nh, 0:64], t1[:, 0:nh, 64:128], ALU.add, [t1], [o])
        else:
            for h in range(nh):
                self.stt("dve", o[:, h, 0:hd], pre[:, h, 0:hd], ss[:, 16 + h:17 + h], gsel(h),
                         ALU.mult, ALU.mult, [pre, ss, gbc], [o])
        return o

    def phase_inproj0(self):
        with Phase(self, "A0") as ph:
            TC = self.TC
            wsrc = self.A("abin").rearrange("(k p) n -> p k n", p=128)
            RW = [self.R("abin")]
            w_q = ph.sb("w_q", [128, KC, 1088], BF16)
            w_ab = ph.sb("w_ab", [128, KC, 32], BF16)
            wzp = ph.pool("wz", [128, KC, 512], BF16, 1)
            self.ld(w_q[:], wsrc[:, :, 0:1088], RW, [w_q])
            self.ld(w_ab[:], wsrc[:, :, 5184:5216], RW, [w_ab])
            uq = ph.sb("uq", [128, 4, 1536], BF16); ukv = ph.sb("ukv", [128, 4, 2048], BF16)
            self.ld(uq[:], self.A("uq").rearrange("(k p) n -> p k n", p=128), [self.R("uq")], [uq])
            self.ld(ukv[:], self.A("ukv").rearrange("(k p) n -> p k n", p=128), [self.R("ukv")], [ukv])
            cst = {}
            for nm, w in (("qa_g", 512), ("kva_g", 512), ("q_g", 192), ("k_g", 192), ("gdn_alog", 16), ("gdn_dtb", 16)):
                cst[nm] = ph.sb(nm, [128, w])
                self.ld(cst[nm][:], self.A(nm), [], [cst[nm]])
            ealog = ph.sb("ealog", [128, 16])
            self.act(ealog[:], cst["gdn_alog"][:], AF.Exp, [cst["gdn_alog"]], [ealog])
            hbp = ph.pool("hb", [128, KC, 512], BF16, 1)
            wfp = ph.pool("wf", [128, KC, 128], BF16, 2)
            psA = ph.psum("psA", [128, 512], F32, 4)
            psT = ph.psum("psT", [128, 1024], BF16, 2)
            fo = ph.pool("fo", [128, 512], F32, 2)
            jk5 = ph.pool("jk5", [128, 512], BF16, 2)
            st4 = ph.pool("st4", [128, 4], F32, 4)
            anb = ph.pool("anb", [128, 512], BF16, 2)
            anT = ph.pool("anT", [128, 4, 128], BF16, 4)
            krp = ph.pool("kr", [128, 64], F32, 2)
            qpre = ph.pool("qpre", [128, 8, 192], F32, 1)
            kvpre = ph.pool("kvpre", [128, 8, 256], F32, 1)
            kpre = ph.pool("kpre", [128, 8, 192], F32, 1)
            hpools = (ph.pool("hj", [128, 8, 192], BF16, 1), ph.pool("hss", [128, 32], F32, 4),
                      ph.pool("ho", [128, 8, 192], BF16, 2), ph.pool("ht", [128, 8, 192], F32, 2))
            vvp = ph.pool("vv", [128, 8, 130], BF16, 2)
            tn = ph.pool("tn", [128, 8, 128], BF16, 2)
            trp = ph.pool("trp", [64, 8, 128], BF16, 2)
            zp = ph.pool("zp", [128, 512], BF16, 2)
            gbp = ph.pool("gbp", [128, 32], F32, 2)
            g1 = ph.pool("g1", [128, 16], F32, 4)
            rc = ph.pool("rc", [128, 64], F32, 2); rs = ph.pool("rs", [128, 64], F32, 2)
            ev = 0
            for (t0, ntok, isctx) in self.blocks(512):
                hb = hbp.get()
                self.ld(hb[:, :, 0:ntok], self.A("hT")[:, :, t0:t0 + ntok], [self.R("hT")], [hb])
                for ch in range(24):
                    w = wfp.get()
                    self.ld(w[:], wsrc[:, :, 1088 + ch * 128:1088 + (ch + 1) * 128], RW, [w])
                    p = psA.get()
                    for k in range(KC):
                        self.mm(p[:, 0:ntok], w[:, k, :], hb[:, k, 0:ntok], k == 0, k == KC - 1, [w, hb], [p])
                    f = fo.get()
                    self.cp("act" if ch % 2 else "dve", f[:, 0:ntok], p[:, 0:ntok], [p], [f])
                    self.st(self.A("gqkv")[ch][:, t0:t0 + ntok], f[:, 0:ntok], [f], [self.R("gqkv")])
                for cb in range(2):
                    wz = wzp.get()
                    self.ld(wz[:], wsrc[:, :, 4160 + cb * 512:4160 + (cb + 1) * 512], RW, [wz])
                    for ti in range(ntok // 128):
                        p = psA.get()
                        for k in range(KC):
                            self.mm(p[:, 0:512], hb[:, k, ti * 128:(ti + 1) * 128], wz[:, k, :], k == 0, k == KC - 1, [hb, wz], [p])
                        z = zp.get()
                        self.act(z[:, 0:512], p[:, 0:512], AF.Silu, [p], [z])
                        self.st(self.A("zs")[t0 + ti * 128:t0 + (ti + 1) * 128, cb * 512:(cb + 1) * 512], z[:, 0:512], [z], [self.R("zs")])
                for ti in range(ntok // 128):
                    tk = t0 + ti * 128
                    sl = slice(ti * 128, (ti + 1) * 128)

                    def lin(wt, c0, wdt, kc=KC, src=hb, ssl=sl):
                        p = psA.get()
                        for k in range(kc):
                            self.mm(p[:, 0:wdt], src[:, k, ssl], wt[:, k, c0:c0 + wdt], k == 0, k == kc - 1, [src, wt], [p])
                        return p

                    def norm_T(p, gname):
                        j = jk5.get(); s4 = st4.get(); a = anb.get(); aT = anT.get()
                        self.act(j[:], p[:, 0:512], AF.Square, [p], [j, s4], accum=s4[:, 0:1])
                        self.rstd(s4[:, 1:2], s4[:, 0:1], 512, [s4], [s4])
                        self.stt("dve", a[:], p[:, 0:512], s4[:, 1:2], cst[gname][:], ALU.mult, ALU.mult, [p, s4, cst[gname]], [a])
                        pt = psT.get()
                        for c in range(4):
                            self.tr(pt[:, c * 128:(c + 1) * 128], a[:, c * 128:(c + 1) * 128], self.identb[:], [a, self.identb], [pt])
                        self.cp("act", aT[:].rearrange("p c t -> p (c t)"), pt[:, 0:512], [pt], [aT])
                        return aT
                    qaT = norm_T(lin(w_q, 0, 512), "qa_g")
                    kvaT = norm_T(lin(w_q, 512, 512), "kva_g")
                    pk = lin(w_q, 1024, 64)
                    kr = krp.get()
                    self.cp("dve", kr[:], pk[:, 0:64], [pk], [kr])
                    qp = qpre.get()
                    for cb in range(3):
                        p = lin(uq, cb * 512, 512, 4, qaT, slice(0, 128))
                        self.cp("act" if cb % 2 else "dve", qp[:].rearrange("p h d -> p (h d)")[:, cb * 512:(cb + 1) * 512], p[:, 0:512], [p], [qp])
                    kvp = kvpre.get()
                    for cb in range(4):
                        p = lin(ukv, cb * 512, 512, 4, kvaT, slice(0, 128))
                        self.cp("act" if cb % 2 else "dve", kvp[:].rearrange("p h d -> p (h d)")[:, cb * 512:(cb + 1) * 512], p[:, 0:512], [p], [kvp])
                    kp = kpre.get()
                    self.cp("pool", kp[:, :, 0:128], kvp[:, :, 0:128], [kvp], [kp])
                    self.cp("pool", kp[:, :, 128:192], self.bcast(kr[:, 0:64], 1, 8), [kr], [kp])
                    v = vvp.get()
                    self.cp("act", v[:, :, 0:128], kvp[:, :, 128:256], [kvp], [v])
                    self.memset("pool", v[:, :, 128:130], 1.0, [v])
                    self.st(self.A("vv")[tk:tk + 128], v[:], [v], [self.R("vv")])
                    cs = None
                    if not isctx:
                        C = rc.get(); S = rs.get()
                        self.ld(C[:], self.A("ropeC")[tk - TC:tk - TC + 128, :], [], [C])
                        self.ld(S[:], self.A("ropeS")[tk - TC:tk - TC + 128, :], [], [S])
                        cs = (C, S)
                    for (pre, gname, dn, dr) in ((qp, "q_g", "qT", "qTr"), (kp, "k_g", "kT", "kTr")):
                        o = self.head_norm_rope(ph, pre, 8, 192, cst[gname], not isctx, cs, hpools, dn)
                        pt = psT.get()
                        for h in range(8):
                            self.tr(pt[:, h * 128:(h + 1) * 128], o[:, h, 0:128], self.identb[:], [o, self.identb], [pt])
                        a = tn.get()
                        self.cp("act", a[:].rearrange("p h t -> p (h t)"), pt[:, 0:1024], [pt], [a])
                        self.st(self.A(dn)[0:8].rearrange("g p t -> p g t")[:, :, tk:tk + 128], a[:], [a], [self.R(dn)])
                        pt2 = psT.get()
                        for h in range(8):
                            self.tr(pt2[0:64, h * 128:(h + 1) * 128], o[:, h, 128:192], self.identb[:], [o, self.identb], [pt2])
                        a2 = trp.get()
                        self.cp("dve", a2[:].rearrange("p h t -> p (h t)"), pt2[0:64, 0:1024], [pt2], [a2])
                        self.st(self.A(dr).rearrange("g p t -> p g t")[:, :, tk:tk + 128], a2[:], [a2], [self.R(dr)])
                    p = lin(w_ab, 0, 32)
                    gb = gbp.get(); t1 = g1.get(); t2 = g1.get()
                    self.tt("dve", t1[:], p[:, 0:16], cst["gdn_dtb"][:], ALU.add, [p, cst["gdn_dtb"]], [t1])
                    self.act(t2[:], t1[:], AF.Exp, [t1], [t2])
                    self.act(t2[:], t2[:], AF.Ln, [t2], [t2], bias=1.0)
                    self.stt("dve", gb[:, 0:16], t2[:], -1.0, ealog[:], ALU.mult, ALU.mult, [t2, ealog], [gb])
                    t3 = g1.get()
                    self.act(t3[:], p[:, 16:32], AF.Exp, [p], [t3], scale=-1.0)
                    self.ts("dve", t3[:], t3[:], 1.0, None, ALU.add, None, [t3], [t3])
                    self.recip(gb[:, 16:32], t3[:], [t3], [gb])
                    self.st(self.A("gb")[tk:tk + 128, :], gb[:], [gb], [self.R("gb")])

    def phase_attn(self, name, G, dk_main, has_rope_part, vmap, scale, with_ctx, sink):
        with Phase(self, name) as ph:
            T, TC, NT = self.T, self.TC, self.NT
            ktp = ph.pool("kt", [128, T], BF16, 2)
            krp = ph.pool("ktr", [64, T], BF16, 2) if has_rope_part else None
            vp = ph.pool("v", [128, NT, 130], BF16, 2)
            qp = ph.pool("q", [128, 512], BF16, 2)
            qrp = ph.pool("qr", [64, 512], BF16, 2) if has_rope_part else None
            psS = ph.psum("S", [128, 512], F32, 4)
            psO = [ph.psum("O%d" % i, [128, 512], F32, 1).get() for i in range(4)]
            ptp = ph.pool("pt", [128, 512], BF16, 4)
            rcp = ph.pool("rc", [128, 1], F32, 4)
            op = ph.pool("o", [128, 128], F32, 4)
            qblocks = []
            if with_ctx:
                for (t0, n, c) in self.blocks(512):
                    if c:
                        qblocks.append((t0, n, TC))
            for (t0, n, c) in self.blocks(512):
                if not c:
                    qblocks.append((t0, n, T))
            vsrc = self.A("vv").rearrange("(n p) h c -> p n h c", p=128)
            for g in range(G):
                kt = ktp.get()
                self.ld(kt[0:dk_main, :], self.A("kT")[g][0:dk_main, :], [self.R("kT")], [kt])
                if has_rope_part:
                    kr = krp.get()
                    self.ld(kr[:], self.A("kTr")[g], [self.R("kTr")], [kr])
                v = vp.get()
                self.ld(v[:], vsrc[:, :, vmap(g), :], [self.R("vv")], [v])
                for (q0, nq, kend) in qblocks:
                    q = qp.get()
                    self.ld(q[0:dk_main, 0:nq], self.A("qT")[g][0:dk_main, q0:q0 + nq], [self.R("qT")], [q])
                    if has_rope_part:
                        qr = qrp.get()
                        self.ld(qr[:, 0:nq], self.A("qTr")[g][:, q0:q0 + nq], [self.R("qTr")], [qr])
                    nkt = kend // 128
                    pts = {}

                    def pv(kti):
                        pt = pts.pop(kti)
                        for qs in range(nq // 128):
                            self.mm(psO[qs][:, 0:130], pt[:, qs * 128:(qs + 1) * 128], v[:, kti, :], kti == 0, kti == nkt - 1,
                                    [pt, v], [psO[qs]])
                    for kti in range(nkt):
                        ks = slice(kti * 128, (kti + 1) * 128)
                        S = psS.get()
                        self.mm(S[:, 0:nq], kt[0:dk_main, ks], q[0:dk_main, 0:nq], True, not has_rope_part, [kt, q], [S])
                        if has_rope_part:
                            self.mm(S[:, 0:nq], kr[:, ks], qr[:, 0:nq], False, True, [kr, qr], [S])
                        pt = ptp.get()
                        self.act(pt[:, 0:nq], S[:, 0:nq], AF.Exp, [S], [pt], scale=scale)
                        pts[kti] = pt
                        if kti >= 1:
                            pv(kti - 1)
                    pv(nkt - 1)
                    for qs in range(nq // 128):
                        r = rcp.get(); o = op.get()
                        self.recip(r[:], psO[qs][:, 128:129], [psO[qs]], [r])
                        self.ts("dve", o[:], psO[qs][:, 0:128], r[:, 0:1], None, ALU.mult, None, [psO[qs], r], [o])
                        sink(ph, g, q0 + qs * 128, o)

    def sink_mix(self, col0):
        cache = {}

        def sink(ph, g, tk, o):
            if ph not in cache:
                cache[ph] = ph.pool("sinkb", [128, 128], BF16, 4)
            b = cache[ph].get()
            self.cp("pool", b[:], o[:], [o], [b])
            self.st(self.A("mix")[tk:tk + 128, col0 + g * 128:col0 + (g + 1) * 128], b[:], [b], [self.R("mix")], q="act")
        return sink

    def sink_attf(self):
        def sink(ph, g, tk, o):
            self.st(self.A("attf")[tk:tk + 128, g, :], o[:], [o], [self.R("attf")], q="act")
        return sink

    def build(self, stop=None):
        self.declare()
        self.consts()
        self.phase_setup()
        steps = []
        if 0 in self.layers:
            steps += [("M0", lambda: self.phase_mod(0)), ("N00", lambda: self.phase_norm(0, "hT", 0)),
                      ("A0", self.phase_inproj0),
                      ("AT0", lambda: self.phase_attn("AT0", 8, 128, True, lambda g: g, 192 ** -0.5, True, self.sink_mix(0))),
                      ("G1", self.phase_gdn_conv), ("G2", self.phase_gdn_scan),
                      ("O0", lambda: self.phase_outproj(0, "xs", "xb", True)),
                      ("F0", lambda: self.phase_ffn(0, "xb", "xs", True, False))]
        if 1 in self.layers:
            steps += [("M1", lambda: self.phase_mod(1)), ("N01", lambda: self.phase_norm(0, "hT", 1)),
                      ("A1", self.phase_inproj1),
                      ("AT1", lambda: self.phase_attn("AT1", 16, 64, False, lambda g: g // 2, 64 ** -0.5, False, self.sink_attf())),
                      ("L2", self.phase_gla_scan),
                      ("O1", lambda: self.phase_outproj(1, "xs", "xb", False)),
                      ("F1", lambda: self.phase_ffn(1, "xb", "xs", False, True))]
        for nm, fn in steps:
            fn()
            if stop == nm:
                break
        self.es.close()
        return self.nc


def rope_tables(TL):
    f32 = np.float32
    rows = TL // 64
    row = np.repeat(np.arange(rows, dtype=f32), 64)
    col = np.tile(np.arange(64, dtype=f32), rows)
    inv = (f32(10000.0) ** (-np.arange(0, 32, 2, dtype=f32) / f32(32))).astype(f32)
    ar = (row[:, None] * inv[None, :]).astype(f32)
    ac = (col[:, None] * inv[None, :]).astype(f32)
    cr, sr, cc, sc = np.cos(ar), np.sin(ar), np.cos(ac), np.sin(ac)
    C = np.concatenate([cr, cr, cc, cc], axis=1).astype(f32)
    S = np.concatenate([-sr, sr, -sc, sc], axis=1).astype(f32)
    return np.ascontiguousarray(C), np.ascontiguousarray(S)


def const_masks():
    i = np.arange(128)[:, None]
    j = np.arange(128)[None, :]
    m = np.stack([i >= j, i <= j, i > j, i < j, i > j, i < j]).astype(np.float32)
    m[4] *= -1.0
    m[5] *= -1.0
    return np.ascontiguousarray(m)


def prep_core_inputs(inp, b, TL, TC):
    f = np.float32
    c = lambda a: np.ascontiguousarray(a, dtype=f)
    tile = lambda v: c(np.tile(np.asarray(v).reshape(1, -1), (128, 1)))
    fp = lambda v: c(np.asarray(v).reshape(-1, 128).T)
    m = {}
    m["x"] = c(inp["x"][b, :TL]); m["ctx"] = c(inp["ctx"][b, :TC])
    m["cT"] = c(np.stack([fp(inp["c"][b]), fp(inp["c_ctx"])], axis=1))
    m["ada_w"] = c(inp["ada_w"]); m["ada_bT"] = c(np.stack([fp(inp["ada_b"][l]) for l in range(2)]))
    m["gmixT"] = c(np.stack([fp(inp["norm_mix_g"][l]) for l in range(2)]))
    m["gffnT"] = c(np.stack([fp(inp["norm_ffn_g"][l]) for l in range(2)]))
    for k in ("ffn_w_gate", "ffn_w_up", "ffn_w_down"):
        m[k] = c(inp[k])
    m["ab_w_in"] = c(inp["ab_w_in"][0]); m["mla_w_uq"] = c(inp["mla_w_uq"][0]); m["mla_w_ukv"] = c(inp["mla_w_ukv"][0])
    m["ab_w_out"] = c(inp["ab_w_out"][0]); m["cd_w_in"] = c(inp["cd_w_in"][0]); m["cd_w_out"] = c(inp["cd_w_out"][0])
    m["qa_g"] = tile(inp["mla_q_a_norm"][0]); m["kva_g"] = tile(inp["mla_kv_a_norm"][0])
    m["q_g"] = tile(inp["mla_q_norm"][0]); m["k_g"] = tile(inp["mla_k_norm"][0])
    m["gdn_on"] = tile(inp["gdn_out_norm"][0]); m["gdn_alog"] = tile(inp["gdn_a_log"][0]); m["gdn_dtb"] = tile(inp["gdn_dt_bias"][0])
    m["convT"] = c(np.asarray(inp["gdn_conv_w"][0]).T.reshape(24, 128, 5).transpose(1, 0, 2))
    m["dq_g"] = tile(inp["diff_q_norm"][0]); m["dk_g"] = tile(inp["diff_k_norm"][0]); m["dsub_g"] = tile(inp["diff_sub_norm"][0])
    m["gla_on"] = tile(inp["gla_out_norm"][0]); m["dlam"] = c(np.asarray(inp["diff_lambda"][0]).reshape(1, 256))
    gw2 = np.zeros((2, 33, 512), f)
    for d_ in range(2):
        gw2[d_, 16 * d_:16 * d_ + 16] = inp["gla_gate_w2"][0][d_]
        gw2[d_, 32] = inp["gla_gate_b2"][0][d_]
    m["gw2"] = gw2
    m["ropeC"], m["ropeS"] = rope_tables(TL)
    m["ident"] = c(np.eye(128)); m["masks"] = const_masks()
    return m


def phase_gdn_conv(self):
    with Phase(self, "G1") as ph:
        T, TC = self.T, self.TC
        cw = ph.sb("cw", [128, 24, 5])
        self.ld(cw[:], self.A("convT"), [], [cw])
        rp = ph.pool("raw", [128, T], F32, 2)
        ap_ = ph.pool("acc", [128, T], F32, 2)
        yp = ph.pool("y", [128, T], F32, 2)
        sqp = ph.pool("sq", [128, 512], F32, 2)
        rbp = ph.pool("rb", [128, 512], F32, 2)
        pp = ph.psum("pp", [128, 512], F32, 3)
        segs = [(0, TC), (TC, T)]
        for ch in range(24):
            raw = rp.get(); acc = ap_.get(); y = yp.get()
            self.ld(raw[:], self.A("gqkv")[ch], [self.R("gqkv")], [raw])
            for (s0, s1) in segs:
                self.ts("dve", acc[:, s0:s1], raw[:, s0:s1], cw[:, ch, 2:3], None, ALU.mult, None, [raw, cw], [acc])
                for j in (0, 1, 3, 4):
                    sh = j - 2
                    a = max(s0, s0 - sh); b = min(s1, s1 - sh)
                    self.stt("dve", acc[:, a:b], raw[:, a + sh:b + sh], cw[:, ch, j:j + 1], acc[:, a:b], ALU.mult, ALU.add,
                             [raw, cw, acc], [acc])
            self.act(y[:], acc[:], AF.Silu, [acc], [y])
            if ch < 16:
                for c0 in range(0, T, 512):
                    w = min(512, T - c0)
                    sq = sqp.get(); rb = rbp.get(); p = pp.get()
                    self.tt("pool", sq[:, 0:w], y[:, c0:c0 + w], y[:, c0:c0 + w], ALU.mult, [y], [sq])
                    self.mm(p[:, 0:w], self.ones[:], sq[:, 0:w], True, True, [self.ones, sq], [p])
                    self.rstd(rb[:, 0:w], p[:, 0:w], 1.0, [p], [rb])
                    if ch < 8:
                        self.stt("dve", y[:, c0:c0 + w], rb[:, 0:w], 128 ** -0.5, y[:, c0:c0 + w], ALU.mult, ALU.mult, [rb, y], [y])
                    else:
                        self.tt("dve", y[:, c0:c0 + w], rb[:, 0:w], y[:, c0:c0 + w], ALU.mult, [rb, y], [y])
            self.st(self.A("gcv")[ch], y[:], [y], [self.R("gcv")])


def phase_gdn_scan(self):
    with Phase(self, "G2") as ph:
        NT, NTC = self.NT, self.NTC
        M = self.masks
        H = 8
        pp = ph.psum("pp", [128, 512], F32, 8)
        S = [[ph.sb("S%d%d" % (d, h), [128, 128]) for h in range(8)] for d in range(2)]
        for d in range(2):
            for h in range(8):
                self.memset("pool", S[d][h][:], 0.0, [S[d][h]])
        gbp = ph.pool("gb", [128, 32], F32, 2)
        kTp = ph.pool("kT", [128, 8, 128], F32, 2); qTp = ph.pool("qT", [128, 8, 128], F32, 2)
        vTp = ph.pool("vT", [128, 8, 128], F32, 2)
        ktokp = ph.pool("ktok", [128, 8, 128], F32, 2); vtokp = ph.pool("vtok", [128, 8, 128], F32, 2)
        colp = ph.pool("col", [128, 64], F32, 2)
        NB = self.gdn_nb
        dgp = ph.pool("dg", [128, 256], F32, NB)
        sq = lambda nm, n=NB: ph.pool(nm, [128, 128], F32, n)
        t1p, t2p, Yp, Zp, eRp = sq("t1"), sq("t2"), sq("Y"), sq("Z"), sq("eR")
        ndp, tmpp, ndtp, Fp = sq("nd"), sq("tmp"), sq("ndt"), sq("F")
        TUp = ph.pool("TU", [128, 256], F32, 2 * NB); Np = sq("N", 2 * NB)
        Tfp, qkfp, rup, rwp, kep, qdp = sq("Tf"), sq("qkf"), sq("ru"), sq("rw"), sq("ke"), sq("qd")
        wTp, up_, vnp = sq("wT"), sq("u"), sq("vn")
        outp = ph.pool("out", [128, 1024], F32, 2)
        gsrc = self.A("gcv")
        order = {0: list(range(NT)), 1: list(range(NTC - 1, -1, -1)) + list(range(NT - 1, NTC - 1, -1))}
        for step in range(NT):
            for d in range(2):
                n = order[d][step]
                ts_ = slice(n * 128, (n + 1) * 128)
                gb = gbp.get(); kT = kTp.get(); qT = qTp.get(); vT = vTp.get()
                self.ld(gb[:], self.A("gb")[ts_, :], [self.R("gb")], [gb])
                self.ld(qT[:], gsrc[0:8].rearrange("h p t -> p h t")[:, :, ts_], [self.R("gcv")], [qT])
                self.ld(kT[:], gsrc[8:16].rearrange("h p t -> p h t")[:, :, ts_], [self.R("gcv")], [kT])
                self.ld(vT[:], gsrc[16:24].rearrange("h p t -> p h t")[:, :, ts_], [self.R("gcv")], [vT])
                ktok = ktokp.get(); vtok = vtokp.get()
                for (src, dst) in ((kT, ktok), (vT, vtok)):
                    for h4 in range(2):
                        p = pp.get()
                        for hh in range(4):
                            h = h4 * 4 + hh
                            self.tr(p[:, hh * 128:(hh + 1) * 128], src[:, h, :], self.ident[:], [src, self.ident], [p])
                        self.cp("act" if h4 else "dve", dst[:, h4 * 4:(h4 + 1) * 4, :].rearrange("p h t -> p (h t)"), p[:], [p], [dst])
                col = colp.get()
                p = pp.get()
                tri = M[:, 1, :] if d == 0 else M[:, 0, :]
                g_d = gb[:, d * 8:(d + 1) * 8]
                self.mm(p[:, 0:8], tri, g_d, True, True, [M, gb], [p])
                self.mm(p[:, 8:16], self.ones[:], g_d, True, True, [self.ones, gb], [p])
                self.cp("dve", col[:, 0:16], p[:, 0:16], [p], [col])
                self.act(col[:, 16:24], col[:, 0:8], AF.Exp, [col], [col])
                self.tt("dve", col[:, 24:32], col[:, 8:16], col[:, 0:8], ALU.subtract, [col], [col])
                self.act(col[:, 24:32], col[:, 24:32], AF.Exp, [col], [col])
                self.act(col[:, 32:40], col[:, 8:16], AF.Exp, [col], [col])
                beta_d = gb[:, 16 + d * 8:16 + (d + 1) * 8]
                self.tt("dve", col[:, 40:48], beta_d, col[:, 16:24], ALU.mult, [gb, col], [col])
                self.ts("dve", col[:, 48:56], beta_d, -1.0, None, ALU.mult, None, [gb], [col])
                out = outp.get()
                sa = M[:, 2, :] if d == 0 else M[:, 3, :]
                nsat = M[:, 5, :] if d == 0 else M[:, 4, :]
                ib = M[:, 1, :] if d == 0 else M[:, 0, :]
                o1, o2 = (128, 256) if self.gdn_pack else (0, 0)
                for h0 in range(0, H, self.gdn_hg):
                    HS = list(range(h0, h0 + self.gdn_hg))
                    pR = [None] * H; Y = [None] * H; Z = [None] * H; eR = [None] * H
                    for h in HS:
                        dg = dgp.get()
                        self.ts("dve", dg[:, 0:128], self.ident[:], col[:, h:h + 1], None, ALU.mult, None, [self.ident, col], [dg])
                        self.ts("dve", dg[:, 128:256], self.ident[:], beta_d[:, h:h + 1], None, ALU.mult, None, [self.ident, gb], [dg])
                        pR[h] = pp.get()
                        self.mm(pR[h][:, 0:256], self.ones[:], dg[:], True, True, [self.ones, dg], [pR[h]])
                    nd = [None] * H; ndt = [None] * H; Fm = [None] * H
                    for h in HS:
                        gc = col[:, h:h + 1]
                        t1 = t1p.get(); t2 = t2p.get(); Y[h] = Yp.get(); Z[h] = Zp.get(); eR[h] = eRp.get()
                        self.ts("dve", t1[:], pR[h][:, 0:128], gc, 0.0, ALU.subtract, ALU.min, [pR[h], col], [t1])
                        self.ts("dve", t2[:], pR[h][:, 0:128], gc, 0.0, ALU.subtract, ALU.max, [pR[h], col], [t2])
                        self.act(Y[h][:], t1[:], AF.Exp, [t1], [Y[h]])
                        self.act(Z[h][:], t2[:], AF.Exp, [t2], [Z[h]], scale=-1.0)
                        self.act(eR[h][:], pR[h][:, 0:128], AF.Exp, [pR[h]], [eR[h]])
                    for h in HS:
                        nd[h] = ndp.get(); tmp = tmpp.get(); ndt[h] = ndtp.get(); Fm[h] = Fp.get()
                        self.stt("dve", nd[h][:], Z[h][:], col[:, 48 + h:49 + h], sa, ALU.mult, ALU.mult, [Z[h], col, M], [nd[h]])
                        self.tt("dve", tmp[:], Y[h][:], pR[h][:, 128:256], ALU.mult, [Y[h], pR[h]], [tmp])
                        self.tt("pool", ndt[h][:], tmp[:], nsat, ALU.mult, [tmp, M], [ndt[h]])
                        self.tt("pool", Fm[h][:], Y[h][:], ib, ALU.mult, [Y[h], M], [Fm[h]])
                    pG = [None] * H; TU = [None] * H; N = [None] * H
                    qkf = [None] * H; ru = [None] * H; rw = [None] * H; ke = [None] * H; qd = [None] * H
                    for h in HS:
                        pG[h] = pp.get()
                        self.mm(pG[h][:, 0:128], kT[:, h, :], kT[:, h, :], True, True, [kT], [pG[h]])
                        self.mm(pG[h][:, 128:256], kT[:, h, :], qT[:, h, :], True, True, [kT, qT], [pG[h]])
                    for h in HS:
                        TU[h] = TUp.get(); N[h] = Np.get(); qkf[h] = qkfp.get()
                        self.tt("dve", N[h][:], pG[h][:, 0:128], nd[h][:], ALU.mult, [pG[h], nd[h]], [N[h]])
                        self.tt("dve", TU[h][:, 128:256], pG[h][:, 0:128], ndt[h][:], ALU.mult, [pG[h], ndt[h]], [TU[h]])
                        self.cp("pool", TU[h][:, 0:128], self.ident[:], [self.ident], [TU[h]])
                        self.tt("dve", qkf[h][:], pG[h][:, 128:256], Fm[h][:], ALU.mult, [pG[h], Fm[h]], [qkf[h]])
                        ru[h] = rup.get(); rw[h] = rwp.get(); ke[h] = kep.get(); qd[h] = qdp.get()
                        self.act(ru[h][:], vtok[:, h, :], AF.Identity, [vtok, gb], [ru[h]], scale=beta_d[:, h:h + 1])
                        self.act(rw[h][:], ktok[:, h, :], AF.Identity, [ktok, col], [rw[h]], scale=col[:, 40 + h:41 + h])
                        self.act(ke[h][:], ktok[:, h, :], AF.Identity, [ktok, col], [ke[h]], scale=col[:, 24 + h:25 + h])
                        self.tt("pool", qd[h][:], qT[:, h, :], eR[h][:], ALU.mult, [qT, eR[h]], [qd[h]])
                    for k in range(6):
                        pk = [None] * H; pk2 = [None] * H
                        for h in HS:
                            pk[h] = pp.get()
                            pk2[h] = pk[h] if self.gdn_pack else pp.get()
                            self.mm(pk[h][:, 0:256], N[h][:], TU[h][:], True, True, [N[h], TU[h]], [pk[h]])
                            self.mm(pk2[h][:, o2:o2 + 128], TU[h][:, 128:256], N[h][:], True, True, [N[h], TU[h]], [pk2[h]])
                        for h in HS:
                            TU2 = TUp.get(); N2 = Np.get()
                            self.tt("dve", TU2[:, 0:128], TU[h][:, 0:128], pk[h][:, 0:128], ALU.add, [TU[h], pk[h]], [TU2])
                            self.cp("act", TU2[:, 128:256], pk[h][:, 128:256], [pk[h]], [TU2])
                            self.cp("act" if h % 2 else "dve", N2[:], pk2[h][:, o2:o2 + 128], [pk2[h]], [N2])
                            TU[h], N[h] = TU2, N2
                    Tf = [None] * H; pW = [None] * H; wT = [None] * H; u = [None] * H
                    for h in HS:
                        pW[h] = pp.get()
                        self.mm(pW[h][:, 0:128], N[h][:], TU[h][:, 0:128], True, True, [N[h], TU[h]], [pW[h]])
                    for h in HS:
                        Tf[h] = Tfp.get()
                        self.tt("dve", Tf[h][:], TU[h][:, 0:128], pW[h][:, 0:128], ALU.add, [TU[h], pW[h]], [Tf[h]])
                    pW2 = [None] * H; pW3 = [None] * H
                    for h in HS:
                        pW2[h] = pW[h] if self.gdn_pack else pp.get()
                        pW3[h] = pW[h] if self.gdn_pack else pp.get()
                        self.mm(pW2[h][:, o1:o1 + 128], rw[h][:], Tf[h][:], True, True, [rw[h], Tf[h]], [pW2[h]])
                        self.mm(pW3[h][:, o2:o2 + 128], Tf[h][:], ru[h][:], True, True, [Tf[h], ru[h]], [pW3[h]])
                    for h in HS:
                        wT[h] = wTp.get(); u[h] = up_.get()
                        self.cp("act", wT[h][:], pW2[h][:, o1:o1 + 128], [pW2[h]], [wT[h]])
                        self.cp("dve", u[h][:], pW3[h][:, o2:o2 + 128], [pW3[h]], [u[h]])
                    pr_ = [None] * H; vn = [None] * H
                    for h in HS:
                        pr_[h] = pp.get()
                        self.mm(pr_[h][:, 0:128], wT[h][:], S[d][h][:], True, True, [wT[h], S[d][h]], [pr_[h]])
                    for h in HS:
                        vn[h] = vnp.get()
                        self.tt("dve", vn[h][:], u[h][:], pr_[h][:, 0:128], ALU.subtract, [u[h], pr_[h]], [vn[h]])
                    pq = [None] * H; pq2 = [None] * H
                    for h in HS:
                        St = S[d][h]
                        pq[h] = pp.get()
                        pq2[h] = pr_[h] if self.gdn_pack else pp.get()
                        self.mm(pq[h][:, 0:128], qd[h][:], St[:], True, False, [qd[h], St], [pq[h]])
                        self.mm(pq[h][:, 0:128], qkf[h][:], vn[h][:], False, True, [qkf[h], vn[h]], [pq[h]])
                        self.mm(pq2[h][:, o1:o1 + 128], ke[h][:], vn[h][:], True, True, [ke[h], vn[h]], [pq2[h]])
                    for h in HS:
                        St = S[d][h]
                        self.cp("act", out[:, h * 128:(h + 1) * 128], pq[h][:, 0:128], [pq[h]], [out])
                        self.stt("dve", St[:], St[:], col[:, 32 + h:33 + h], pq2[h][:, o1:o1 + 128], ALU.mult, ALU.add, [St, col, pq2[h]], [St])
                self.st(self.A("rec")[d][ts_, :], out[:], [out], [self.R("rec")])


Kern.phase_gdn_conv = phase_gdn_conv
Kern.phase_gdn_scan = phase_gdn_scan


def norm_tile(self, ph, x, s, which, pools, dstname, n):
    jk, xn, st_, pt, hp = pools
    sh_i = 0 if which == 0 else 3
    j = jk.get(); xx = xn.get(); stt_ = st_.get(); p = pt.get(); h = hp.get()
    self.act(j[:], x[:], AF.Square, [x], [j, stt_], accum=stt_[:, 0:1])
    self.rstd(stt_[:, 1:2], stt_[:, 0:1], D, [stt_], [stt_])
    self.ts("dve", xx[:], x[:], stt_[:, 1:2], None, ALU.mult, None, [x, stt_], [xx])
    for c in range(KC):
        self.tr(p[:, c * 128:(c + 1) * 128], xx[:, c * 128:(c + 1) * 128], self.identb[:], [xx, self.identb], [p])
    for c in range(KC):
        gcol = self.gm[:, which, s, c:c + 1]
        scol = self.mod[:, s, sh_i * KC + c:sh_i * KC + c + 1]
        if c % 2 == 0:
            self.act(h[:, c, :], p[:, c * 128:(c + 1) * 128], AF.Identity, [p, self.gm, self.mod], [h], scale=gcol, bias=scol)
        else:
            self.ts("dve", h[:, c, :], p[:, c * 128:(c + 1) * 128], gcol, scol, ALU.mult, ALU.add, [p, self.gm, self.mod], [h])
    self.st(self.A(dstname)[:, :, n * 128:(n + 1) * 128], h[:], [h], [self.R(dstname)])


def phase_outproj(self, l, xin, xout, need_ctx):
    with Phase(self, "O%d" % l) as ph:
        NT, NTC = self.NT, self.NTC
        wname = "about" if l == 0 else "cdout"
        wo = ph.sb("wo", [128, KC, D], BF16)
        self.ld(wo[:], self.A(wname).rearrange("(k p) n -> p k n", p=128), [self.R(wname)], [wo])
        nh, dv = (8, 128) if l == 0 else (4, 256)
        gon = ph.sb("gon", [128, dv])
        self.ld(gon[:], self.A("gdn_on" if l == 0 else "gla_on"), [], [gon])
        gbc = [ph.sb("gbc%d" % s, [128, D]) for s in range(2)]
        for s in range(2):
            self.ld(gbc[s][:], self.A("gbcd")[0, s], [self.R("gbcd")], [gbc[s]])
        r0p = ph.pool("r0", [128, 1024], F32, 2); r1p = ph.pool("r1", [128, 1024], F32, 2)
        zp = ph.pool("z", [128, 1024], BF16, 2)
        jkp = ph.pool("jk", [128, 1024], BF16, 1); ssp = ph.pool("ss", [128, 16], F32, 2)
        mxp = ph.pool("mx", [128, D], BF16, 2)
        mtp = ph.pool("mT", [128, KC, 128], BF16, 2)
        xp = ph.pool("x", [128, D], F32, 2); xop = ph.pool("xo", [128, D], F32, 2)
        tmp = ph.pool("tmp", [128, 512], F32, 2)
        pt = ph.psum("pt", [128, D], BF16, 1)
        pnt = ph.psum("pnt", [128, D], BF16, 1)
        py = ph.psum("py", [128, 512], F32, 3)
        npools = (ph.pool("njk", [128, D], BF16, 1), ph.pool("nxn", [128, D], BF16, 2), ph.pool("nst", [128, 2], F32, 4),
                  pnt, ph.pool("nh", [128, KC, 128], BF16, 2))
        if l == 1:
            lam_init = 0.8 - 0.6 * math.exp(-0.3 * l)
            dl = ph.sb("dl", [1, 256]); pr = ph.sb("pr", [1, 128]); sm = ph.sb("sm", [1, 4])
            nlam = ph.sb("nlam", [128, 1]); gsub = ph.sb("gsub", [128, 128])
            self.ld(dl[:], self.A("dlam"), [], [dl])
            self.ld(gsub[:], self.A("dsub_g"), [], [gsub])
            self.ts("dve", gsub[:], gsub[:], 1.0 - lam_init, None, ALU.mult, None, [gsub], [gsub])
            self.tt("dve", pr[:, 0:64], dl[:, 0:64], dl[:, 64:128], ALU.mult, [dl], [pr])
            self.tt("dve", pr[:, 64:128], dl[:, 128:192], dl[:, 192:256], ALU.mult, [dl], [pr])
            self.red(sm[:, 0:2], pr[:].rearrange("p (a d) -> p a d", a=2), [pr], [sm])
            self.act(sm[:, 0:2], sm[:, 0:2], AF.Exp, [sm], [sm])
            self.tt("dve", sm[:, 2:3], sm[:, 0:1], sm[:, 1:2], ALU.subtract, [sm], [sm])
            self.ts("dve", sm[:, 3:4], sm[:, 2:3], lam_init, None, ALU.add, None, [sm], [sm])
            pl = py.get()
            self.mm(pl[:, 0:1], self.ones[0:1, :], sm[0:1, 3:4], True, True, [self.ones, sm], [pl])
            self.ts("dve", nlam[:], pl[:, 0:1], -1.0, None, ALU.mult, None, [pl], [nlam])
            afp = ph.pool("af", [128, 16, 128], F32, 1); a8p = ph.pool("a8", [128, 8, 128], F32, 1)
            j2p = ph.pool("j2", [128, 8, 128], BF16, 1)
        for n in range(NT):
            if n < NTC and not need_ctx:
                continue
            s = 1 if n < NTC else 0
            ts_ = slice(n * 128, (n + 1) * 128)
            r0 = r0p.get(); r1 = r1p.get(); z = zp.get(); mx = mxp.get(); x = xp.get()
            self.ld(r0[:], self.A("rec")[0][ts_, :], [self.R("rec")], [r0])
            self.ld(r1[:], self.A("rec")[1][ts_, :], [self.R("rec")], [r1])
            self.ld(z[:], self.A("zs")[ts_, :], [self.R("zs")], [z])
            if l == 0:
                self.ld(mx[:, 0:1024], self.A("mix")[ts_, 0:1024], [self.R("mix")], [mx])
            else:
                af = afp.get(); a8 = a8p.get(); j2 = j2p.get(); ss2 = ssp.get()
                self.ld(af[:], self.A("attf")[ts_], [self.R("attf")], [af])
                afv = af[:].rearrange("p (h m) d -> p h m d", m=2)
                self.stt("dve", a8[:], afv[:, :, 1, :], nlam[:, 0:1], afv[:, :, 0, :], ALU.mult, ALU.add, [af, nlam], [a8])
                self.act(j2[:], a8[:], AF.Square, [a8], [j2])
                self.red(ss2[:, 0:8], j2[:], [j2], [ss2])
                self.rstd(ss2[:, 8:16], ss2[:, 0:8], 128, [ss2], [ss2])
                for h in range(8):
                    self.stt("dve", mx[:, h * 128:(h + 1) * 128], a8[:, h, :], ss2[:, 8 + h:9 + h], gsub[:], ALU.mult, ALU.mult,
                             [a8, ss2, gsub], [mx])
            self.ld(x[:], self.A(xin)[ts_, :], [self.R(xin)], [x])
            self.tt("pool", r0[:], r0[:], r1[:], ALU.add, [r0, r1], [r0])
            jk = jkp.get(); ss = ssp.get()
            self.act(jk[:], r0[:], AF.Square, [r0], [jk])
            self.red(ss[:, 0:nh], jk[:].rearrange("p (h d) -> p h d", h=nh), [jk], [ss])
            self.rstd(ss[:, 8:8 + nh], ss[:, 0:nh], dv, [ss], [ss])
            for h in range(nh):
                self.stt("dve", r1[:, h * dv:(h + 1) * dv], r0[:, h * dv:(h + 1) * dv], ss[:, 8 + h:9 + h], gon[:], ALU.mult, ALU.mult,
                         [r0, ss, gon], [r1])
            self.tt("dve", mx[:, 1024:2048], r1[:], z[:], ALU.mult, [r1, z], [mx])
            p = pt.get()
            for c in range(KC):
                self.tr(p[:, c * 128:(c + 1) * 128], mx[:, c * 128:(c + 1) * 128], self.identb[:], [mx, self.identb], [p])
            mT = mtp.get()
            self.cp("act", mT[:].rearrange("p c t -> p (c t)")[:, 0:1024], p[:, 0:1024], [p], [mT])
            self.cp("dve", mT[:].rearrange("p c t -> p (c t)")[:, 1024:2048], p[:, 1024:2048], [p], [mT])
            xo = xop.get()
            for cb in range(4):
                y = py.get()
                for k in range(KC):
                    self.mm(y[:], mT[:, k, :], wo[:, k, cb * 512:(cb + 1) * 512], k == 0, k == KC - 1, [mT, wo], [y])
                t = tmp.get()
                self.tt("dve", t[:], y[:], gbc[s][:, cb * 512:(cb + 1) * 512], ALU.mult, [y, gbc[s]], [t])
                self.tt("pool", xo[:, cb * 512:(cb + 1) * 512], t[:], x[:, cb * 512:(cb + 1) * 512], ALU.add, [t, x], [xo])
            self.st(self.A(xout)[ts_, :], xo[:], [xo], [self.R(xout)], q="act")
            self.norm_tile(ph, xo, s, 1, npools, "h2T", n)


def phase_ffn(self, l, xin, xout, need_ctx, final):
    with Phase(self, "F%d" % l) as ph:
        TC = self.TC
        NJ = FFN_H // 128
        wg = self.A("wg%d" % l).rearrange("(k p) n -> p k n", p=128)
        wu = self.A("wu%d" % l).rearrange("(k p) n -> p k n", p=128)
        wd = self.A("wd%d" % l).rearrange("(j p) n -> p j n", p=128)
        gbc = [ph.sb("gbc%d" % s, [128, D]) for s in range(2)]
        for s in range(2):
            self.ld(gbc[s][:], self.A("gbcd")[1, s], [self.R("gbcd")], [gbc[s]])
        hbp = ph.pool("hb", [128, KC, 512], BF16, 1)
        wgp = ph.pool("wg", [128, KC, 512], BF16, 2); wup = ph.pool("wu", [128, KC, 512], BF16, 2)
        actp = ph.pool("act", [128, NJ, 512], BF16, 1)
        sgp = ph.pool("sg", [128, 512], F32, 2)
        wdp = ph.pool("wd", [128, 11, 512], BF16, 2)
        xp = ph.pool("x", [128, 512], F32, 2); xop = ph.pool("xo", [128, 512], F32, 2); tp = ph.pool("t", [128, 512], F32, 2)
        pg = ph.psum("pg", [128, 512], F32, 2); pu = ph.psum("pu", [128, 512], F32, 2)
        pd = [ph.psum("pd%d" % i, [128, 512], F32, 1).get() for i in range(4)]
        for (t0, ntok, isctx) in self.blocks(512):
            if isctx and not need_ctx:
                continue
            s = 1 if isctx else 0
            hb = hbp.get()
            self.ld(hb[:, :, 0:ntok], self.A("h2T")[:, :, t0:t0 + ntok], [self.R("h2T")], [hb])
            at = actp.get()
            for j4 in range(NJ // 4):
                g_ = wgp.get(); u_ = wup.get()
                self.ld(g_[:], wg[:, :, j4 * 512:(j4 + 1) * 512], [self.R("wg%d" % l)], [g_])
                self.ld(u_[:], wu[:, :, j4 * 512:(j4 + 1) * 512], [self.R("wu%d" % l)], [u_])
                for jj in range(4):
                    j = j4 * 4 + jj
                    a = pg.get(); b = pu.get()
                    for k in range(KC):
                        self.mm(a[:, 0:ntok], g_[:, k, jj * 128:(jj + 1) * 128], hb[:, k, 0:ntok], k == 0, k == KC - 1, [g_, hb], [a])
                    for k in range(KC):
                        self.mm(b[:, 0:ntok], u_[:, k, jj * 128:(jj + 1) * 128], hb[:, k, 0:ntok], k == 0, k == KC - 1, [u_, hb], [b])
                    sg = sgp.get()
                    self.act(sg[:, 0:ntok], a[:, 0:ntok], AF.Silu, [a], [sg])
                    self.tt("dve", at[:, j, 0:ntok], sg[:, 0:ntok], b[:, 0:ntok], ALU.mult, [sg, b], [at])
            ntile = ntok // 128
            for cb in range(4):
                for pc in range(4):
                    w = wdp.get()
                    self.ld(w[:], wd[:, pc * 11:(pc + 1) * 11, cb * 512:(cb + 1) * 512], [self.R("wd%d" % l)], [w])
                    for ti in range(ntile):
                        for jj in range(11):
                            j = pc * 11 + jj
                            self.mm(pd[ti][:], at[:, j, ti * 128:(ti + 1) * 128], w[:, jj, :], j == 0, j == NJ - 1, [at, w], [pd[ti]])
                for ti in range(ntile):
                    tk = t0 + ti * 128
                    x = xp.get(); xo = xop.get(); t = tp.get()
                    self.ld(x[:], self.A(xin)[tk:tk + 128, cb * 512:(cb + 1) * 512], [self.R(xin)], [x])
                    self.tt("dve", t[:], pd[ti][:], gbc[s][:, cb * 512:(cb + 1) * 512], ALU.mult, [pd[ti], gbc[s]], [t])
                    self.tt("pool", xo[:], t[:], x[:], ALU.add, [t, x], [xo])
                    if final:
                        self.st(self.A("out")[tk - TC:tk - TC + 128, cb * 512:(cb + 1) * 512], xo[:], [xo], [self.R("out")], q="act")
                    else:
                        self.st(self.A(xout)[tk:tk + 128, cb * 512:(cb + 1) * 512], xo[:], [xo], [self.R(xout)], q="act")


Kern.norm_tile = norm_tile
Kern.phase_outproj = phase_outproj
Kern.phase_ffn = phase_ffn


def phase_inproj1(self):
    with Phase(self, "A1") as ph:
        TC = self.TC
        wsrc = self.A("cdin").rearrange("(k p) n -> p k n", p=128)
        RW = [self.R("cdin")]
        w_lr = ph.sb("w_lr", [128, KC, 32], BF16)
        self.ld(w_lr[:], wsrc[:, :, 6144:6176], RW, [w_lr])
        cst = {}
        for nm in ("dq_g", "dk_g"):
            cst[nm] = ph.sb(nm, [128, 128])
            self.ld(cst[nm][:], self.A(nm), [], [cst[nm]])
        hbp = ph.pool("hb", [128, KC, 512], BF16, 1)
        wfp = ph.pool("wf", [128, KC, 128], BF16, 2)
        wtp = ph.pool("wt", [128, KC, 512], BF16, 2)
        psA = ph.psum("psA", [128, 512], F32, 4)
        psT = ph.psum("psT", [128, 1024], BF16, 2)
        fo = ph.pool("fo", [128, 512], F32, 3)
        pre = ph.pool("pre", [128, 8, 64], F32, 2)
        hpools = (ph.pool("hj", [128, 8, 64], BF16, 1), ph.pool("hss", [128, 32], F32, 4),
                  ph.pool("ho", [128, 8, 64], BF16, 2), ph.pool("ht", [128, 8, 128], F32, 2))
        vvp = ph.pool("vv", [128, 4, 130], BF16, 2)
        trp = ph.pool("trp", [64, 8, 128], BF16, 2)
        zp = ph.pool("zp", [128, 512], BF16, 2)
        rc = ph.pool("rc", [128, 64], F32, 4); rs = ph.pool("rs", [128, 64], F32, 4)
        for (t0, ntok, isctx) in self.blocks(512):
            hb = hbp.get()
            self.ld(hb[:, :, 0:ntok], self.A("hT")[:, :, t0:t0 + ntok], [self.R("hT")], [hb])
            ntile = ntok // 128
            cs = [None] * ntile
            if not isctx:
                for ti in range(ntile):
                    tk = t0 + ti * 128
                    C = rc.get(); S = rs.get()
                    self.ld(C[:], self.A("ropeC")[tk - TC:tk - TC + 128, :], [], [C])
                    self.ld(S[:], self.A("ropeS")[tk - TC:tk - TC + 128, :], [], [S])
                    cs[ti] = (C, S)
            for ch in range(8):
                w = wfp.get()
                self.ld(w[:], wsrc[:, :, 3072 + ch * 128:3072 + (ch + 1) * 128], RW, [w])
                p = psA.get()
                for k in range(KC):
                    self.mm(p[:, 0:ntok], w[:, k, :], hb[:, k, 0:ntok], k == 0, k == KC - 1, [w, hb], [p])
                f = fo.get()
                if ch < 4:
                    self.act(f[:, 0:ntok], p[:, 0:ntok], AF.Copy, [p], [f], scale=128 ** -0.5)
                    self.st(self.A("glq")[ch][:, t0:t0 + ntok], f[:, 0:ntok], [f], [self.R("glq")])
                else:
                    self.cp("dve", f[:, 0:ntok], p[:, 0:ntok], [p], [f])
                    self.st(self.A("glk")[ch - 4][:, t0:t0 + ntok], f[:, 0:ntok], [f], [self.R("glk")])
            p = psA.get()
            for k in range(KC):
                self.mm(p[0:32, 0:ntok], w_lr[:, k, :], hb[:, k, 0:ntok], k == 0, k == KC - 1, [w_lr, hb], [p])
            f = fo.get()
            self.cp("dve", f[0:32, 0:ntok], p[0:32, 0:ntok], [p], [f])
            self.st(self.A("glr")[:, t0:t0 + ntok], f[0:32, 0:ntok], [f], [self.R("glr")])
            jobs = [("dq", 0, 0), ("dq", 512, 1), ("dk", 1024, 0), ("dk", 1536, 1), ("dv", 2048, 0), ("dv", 2560, 1),
                    ("gk", 3584, 0), ("gv", 4096, 0), ("gv", 4608, 1), ("gg", 5120, 0), ("gg", 5632, 1)]
            for (kind, c0, idx) in jobs:
                wt = wtp.get()
                self.ld(wt[:], wsrc[:, :, c0:c0 + 512], RW, [wt])
                for ti in range(ntile):
                    tk = t0 + ti * 128
                    p = psA.get()
                    for k in range(KC):
                        self.mm(p[:], hb[:, k, ti * 128:(ti + 1) * 128], wt[:, k, :], k == 0, k == KC - 1, [hb, wt], [p])
                    if kind in ("dq", "dk"):
                        pr = pre.get()
                        self.cp("act", pr[:].rearrange("p g d -> p (g d)"), p[:], [p], [pr])
                        gname = "dq_g" if kind == "dq" else "dk_g"
                        g_ = cst[gname]
                        cs_t = cs[ti]
                        o = self.head_norm_rope(ph, pr, 8, 64, g_, not isctx, cs_t, hpools, kind,
                                                gsel=lambda h, g_=g_: g_[:, (h % 2) * 64:(h % 2 + 1) * 64])
                        pt = psT.get()
                        for g in range(8):
                            self.tr(pt[0:64, g * 128:(g + 1) * 128], o[:, g, :], self.identb[:], [o, self.identb], [pt])
                        a2 = trp.get()
                        self.cp("dve", a2[:].rearrange("p h t -> p (h t)"), pt[0:64, 0:1024], [pt], [a2])
                        dn = "qT" if kind == "dq" else "kT"
                        self.st(self.A(dn)[idx * 8:(idx + 1) * 8].rearrange("g p t -> p g t")[0:64, :, tk:tk + 128], a2[:], [a2], [self.R(dn)])
                    elif kind == "dv":
                        v = vvp.get()
                        self.cp("act", v[:, :, 0:128], p[:].rearrange("p (h d) -> p h d", h=4), [p], [v])
                        self.memset("pool", v[:, :, 128:130], 1.0, [v])
                        self.st(self.A("vv")[tk:tk + 128, idx * 4:(idx + 1) * 4, :], v[:], [v], [self.R("vv")])
                    elif kind == "gg":
                        z = zp.get()
                        self.act(z[:], p[:], AF.Silu, [p], [z])
                        self.st(self.A("zs")[tk:tk + 128, idx * 512:(idx + 1) * 512], z[:], [z], [self.R("zs")])
                    else:
                        f = fo.get()
                        self.cp("act" if ti % 2 else "dve", f[:], p[:], [p], [f])
                        if kind == "gk":
                            self.st(self.A("glkt")[tk:tk + 128, :], f[:], [f], [self.R("glkt")])
                        else:
                            self.st(self.A("glv")[tk:tk + 128, idx * 512:(idx + 1) * 512], f[:], [f], [self.R("glv")])


def phase_gla_scan(self):
    with Phase(self, "L2") as ph:
        NT, NTC = self.NT, self.NTC
        M = self.masks
        pp = ph.psum("pp", [128, 512], F32, 7)
        S = [[ph.sb("S%d%d" % (d, h), [128, 256]) for h in range(4)] for d in range(2)]
        for d in range(2):
            for h in range(4):
                self.memset("pool", S[d][h][:], 0.0, [S[d][h]])
        gw = ph.sb("gw", [33, 2, 512])
        self.ld(gw[:], self.A("gw2").rearrange("d k n -> k d n"), [], [gw])
        lrp = ph.pool("lr", [33, 128], F32, 2)
        qTp = ph.pool("qT", [128, 4, 128], F32, 2); kTp = ph.pool("kT", [128, 4, 128], F32, 2)
        ktp = ph.pool("kt", [128, 512], F32, 2); vp = ph.pool("v", [128, 1024], F32, 2)
        glp = ph.pool("gl", [128, 512], F32, 2)
        e1p = ph.pool("e1", [128, 128], F32, 2); e2p = ph.pool("e2", [128, 128], F32, 2)
        qtp = ph.pool("qt", [128, 128], F32, 2); ktlp = ph.pool("ktl", [128, 128], F32, 2)
        amp = ph.pool("am", [128, 128], F32, 2); dfp = ph.pool("df", [128, 128], F32, 2); kep = ph.pool("ke", [128, 128], F32, 2)
        ecp = ph.pool("ec", [128, 1], F32, 4)
        outp = ph.pool("out", [128, 1024], F32, 2)
        order = {0: list(range(NT)), 1: list(range(NTC - 1, -1, -1)) + list(range(NT - 1, NTC - 1, -1))}
        for step in range(NT):
            for d in range(2):
                n = order[d][step]
                with_out = n >= NTC
                ts_ = slice(n * 128, (n + 1) * 128)
                lr = lrp.get(); qT = qTp.get(); kT = kTp.get(); kt = ktp.get(); v = vp.get()
                self.ld(lr[0:32, :], self.A("glr")[:, ts_], [self.R("glr")], [lr])
                self.memset("pool", lr[32:33, :], 1.0, [lr])
                if with_out:
                    self.ld(qT[:], self.A("glq").rearrange("h p t -> p h t")[:, :, ts_], [self.R("glq")], [qT])
                    self.ld(kT[:], self.A("glk").rearrange("h p t -> p h t")[:, :, ts_], [self.R("glk")], [kT])
                self.ld(kt[:], self.A("glkt")[ts_, :], [self.R("glkt")], [kt])
                self.ld(v[:], self.A("glv")[ts_, :], [self.R("glv")], [v])
                pl = pp.get()
                self.mm(pl[:], lr[:], gw[:, d, :], True, True, [lr, gw], [pl])
                gl = glp.get()
                self.act(gl[:], pl[:], AF.Exp, [pl], [gl], scale=-1.0)
                self.act(gl[:], gl[:], AF.Ln, [gl], [gl], bias=1.0)
                self.ts("dve", gl[:], gl[:], -1.0 / 16.0, None, ALU.mult, None, [gl], [gl])
                tri = M[:, 1, :] if d == 0 else M[:, 0, :]
                out = outp.get() if with_out else None
                for h in range(4):
                    glh = gl[:, h * 128:(h + 1) * 128]
                    pc = pp.get()
                    self.mm(pc[:, 0:128], tri, glh, True, True, [M, gl], [pc])
                    self.mm(pc[:, 128:256], self.ones[:], glh, True, True, [self.ones, gl], [pc])
                    self.mm(pc[:, 384:385], glh, self.ones[:, 0:1], True, True, [gl, self.ones], [pc])
                    if with_out:
                        self.mm(pc[:, 256:384], glh, tri, True, True, [gl, M], [pc])
                    df = dfp.get(); ke = kep.get()
                    cumc = e1p.get()
                    self.cp("act", cumc[:], pc[:, 0:128], [pc], [cumc])
                    self.tt("dve", df[:], pc[:, 128:256], cumc[:], ALU.subtract, [pc, cumc], [df])
                    self.act(df[:], df[:], AF.Exp, [df], [df])
                    self.tt("pool", ke[:], kt[:, h * 128:(h + 1) * 128], df[:], ALU.mult, [kt, df], [ke])
                    ec = ecp.get()
                    self.act(ec[:], pc[:, 384:385], AF.Exp, [pc], [ec])
                    St = S[d][h]
                    vh = v[:, h * 256:(h + 1) * 256]
                    if with_out:
                        e1 = e1p.get(); e2 = e2p.get(); qt = qtp.get(); ktl = ktlp.get()
                        self.act(e1[:], pc[:, 256:384], AF.Exp, [pc], [e1])
                        self.act(e2[:], pc[:, 256:384], AF.Exp, [pc], [e2], scale=-1.0)
                        self.tt("dve", qt[:], qT[:, h, :], e1[:], ALU.mult, [qT, e1], [qt])
                        self.tt("pool", ktl[:], kT[:, h, :], e2[:], ALU.mult, [kT, e2], [ktl])
                        pa = pp.get()
                        self.mm(pa[:, 0:128], ktl[:], qt[:], True, True, [ktl, qt], [pa])
                        am = amp.get()
                        ib = M[:, 1, :] if d == 0 else M[:, 0, :]
                        self.tt("dve", am[:], pa[:, 0:128], ib, ALU.mult, [pa, M], [am])
                        po = pp.get()
                        self.mm(po[:, 0:256], qt[:], St[:], True, False, [qt, St], [po])
                        self.mm(po[:, 0:256], am[:], vh, False, True, [am, v], [po])
                        self.cp("act", out[:, h * 256:(h + 1) * 256], po[:, 0:256], [po], [out])
                    pk = pp.get()
                    self.mm(pk[:, 0:256], ke[:], vh, True, True, [ke, v], [pk])
                    self.stt("dve", St[:], St[:], ec[:, 0:1], pk[:, 0:256], ALU.mult, ALU.add, [St, ec, pk], [St])
                if with_out:
                    self.st(self.A("rec")[d][ts_, :], out[:], [out], [self.R("rec")])


Kern.phase_inproj1 = phase_inproj1
Kern.phase_gla_scan = phase_gla_scan


_CACHE = {}


def kernel(**inputs):
    TL, TC, B = 4096, 256, 4
    inp = {k: np.asarray(v) for k, v in inputs.items()}
    if "nc" not in _CACHE:
        kk = Kern(TL, TC)
        _CACHE["nc"] = kk.build()
        _CACHE["names"] = list(kk.in_names)
    nc = _CACHE["nc"]
    names = _CACHE["names"]
    maps = []
    for b in range(B):
        m = prep_core_inputs(inp, b, TL, TC)
        maps.append({k: m[k] for k in names})
    in_maps = [maps[c % B] for c in range(8)]
    res = run_bass_kernel_spmd(nc, in_maps, core_ids=list(range(8)))
    out = np.stack([np.asarray(res.results[b]["out"], dtype=np.float32) for b in range(B)], axis=0)
    return out
```
